# Optimizing a Trainium2 kernel written in Bass

```python
import jax, jax.numpy as jnp
from jax import lax
import numpy as np

D_MODEL = 1024
BATCH = 16
SEQ = 4096
DEPTH = 1

PLE_DIM = 256
MIX_WIDTH = D_MODEL
CONV_WIDTH = MIX_WIDTH // 2
GROUP_DIM = 64
N_CONV_GROUPS = CONV_WIDTH // GROUP_DIM
CONV_K = 3
ATTN_WIDTH = MIX_WIDTH - CONV_WIDTH
HEAD_DIM = GROUP_DIM
N_ATTN_HEADS = ATTN_WIDTH // HEAD_DIM
D_FF = -(-8 * D_MODEL // (3 * 256)) * 256
Q_BLOCK = 128
EPS = 1e-6
IN_COLS = 3 * CONV_WIDTH + 3 * ATTN_WIDTH + N_ATTN_HEADS

kernel_name = "hybrid_conv_forgetting_attn_ple_layer"


def rms_norm(x, g):
    xf = x.astype(jnp.float32)
    y = xf * lax.rsqrt(jnp.mean(xf * xf, axis=-1, keepdims=True) + EPS)
    return (y * g.astype(jnp.float32)).astype(x.dtype)


def group_rms_norm(y, g):
    B, S, W = y.shape
    yf = y.astype(jnp.float32).reshape(B, S, W // GROUP_DIM, GROUP_DIM)
    yf = yf * lax.rsqrt(jnp.mean(yf * yf, axis=-1, keepdims=True) + EPS)
    return (yf.reshape(B, S, W) * g.astype(jnp.float32)).astype(y.dtype)


def causal_depthwise_conv(u, w):
    S = u.shape[1]
    u_pad = jnp.pad(u, ((0, 0), (CONV_K - 1, 0), (0, 0)))
    return sum(w[j] * u_pad[:, j:j + S] for j in range(CONV_K))


def forgetting_attention(q, k, v, log_f):
    B, S, H, dh = q.shape
    nb = S // Q_BLOCK
    c = jnp.cumsum(log_f, axis=1)
    c_k = c.transpose(0, 2, 1)
    qf = q.astype(jnp.float32) * (dh ** -0.5)
    kf = k.astype(jnp.float32)
    vf = v.astype(jnp.float32)
    q_blocks = qf.reshape(B, nb, Q_BLOCK, H, dh).transpose(1, 0, 2, 3, 4)
    c_blocks = c.reshape(B, nb, Q_BLOCK, H).transpose(1, 0, 3, 2)
    k_pos = jnp.arange(S)

    def one_block(args):
        q_blk, c_q, blk = args
        s = jnp.einsum('bqhd,bkhd->bhqk', q_blk, kf)
        s = s + c_q[..., :, None] - c_k[:, :, None, :]
        q_pos = blk * Q_BLOCK + jnp.arange(Q_BLOCK)
        causal = k_pos[None, :] <= q_pos[:, None]
        s = jnp.where(causal, s, -jnp.inf)
        pr = jax.nn.softmax(s, axis=-1)
        return jnp.einsum('bhqk,bkhd->bqhd', pr, vf)

    out = lax.map(one_block, (q_blocks, c_blocks, jnp.arange(nb)))
    return out.transpose(1, 0, 2, 3, 4).reshape(B, S, H, dh).astype(v.dtype)


def setup_inputs(seed: int = 0) -> dict:
    key = jax.random.key(seed)
    ks = jax.random.split(key, 20)
    f32 = jnp.float32
    nrm = lambda k, shape, fan_in: jax.random.normal(k, shape, f32) * (fan_in ** -0.5)
    gain = lambda k, shape: 1.0 + 0.05 * jax.random.normal(k, shape, f32)
    x = jax.random.normal(ks[0], (BATCH, SEQ, D_MODEL), f32)
    p = jax.random.normal(ks[1], (DEPTH, BATCH, SEQ, PLE_DIM), f32)
    mix_norm = gain(ks[2], (DEPTH, D_MODEL))
    w_in = nrm(ks[3], (DEPTH, D_MODEL, IN_COLS), D_MODEL)
    b_f = (jnp.linspace(1.0, 6.0, N_ATTN_HEADS, dtype=f32)[None, :]
           + 0.01 * jax.random.normal(ks[4], (DEPTH, N_ATTN_HEADS), f32))
    conv_w = nrm(ks[5], (DEPTH, CONV_K, CONV_WIDTH), CONV_K)
    mix_out_norm = gain(ks[6], (DEPTH, MIX_WIDTH))
    w_o = nrm(ks[7], (DEPTH, MIX_WIDTH, D_MODEL), MIX_WIDTH)
    ffn_norm = gain(ks[8], (DEPTH, D_MODEL))
    w_gate_up = nrm(ks[9], (DEPTH, D_MODEL, 2 * D_FF), D_MODEL)
    w_down = nrm(ks[10], (DEPTH, D_FF, D_MODEL), D_FF)
    ple_norm = gain(ks[11], (DEPTH, D_MODEL))
    w_ple_gate = nrm(ks[12], (DEPTH, D_MODEL, D_MODEL), D_MODEL)
    b_ple_gate = 0.01 * jax.random.normal(ks[13], (DEPTH, D_MODEL), f32)
    w_ple_proj = nrm(ks[14], (DEPTH, PLE_DIM, D_MODEL), PLE_DIM)
    final_norm = gain(ks[15], (D_MODEL,))
    return {"x": x, "p": p, "mix_norm": mix_norm, "w_in": w_in, "b_f": b_f,
            "conv_w": conv_w, "mix_out_norm": mix_out_norm, "w_o": w_o,
            "ffn_norm": ffn_norm, "w_gate_up": w_gate_up, "w_down": w_down,
            "ple_norm": ple_norm, "w_ple_gate": w_ple_gate, "b_ple_gate": b_ple_gate,
            "w_ple_proj": w_ple_proj, "final_norm": final_norm}


def reference(x, p, mix_norm, w_in, b_f, conv_w, mix_out_norm, w_o, ffn_norm,
              w_gate_up, w_down, ple_norm, w_ple_gate, b_ple_gate, w_ple_proj,
              final_norm):
    B, S, _ = x.shape
    o_b = 0
    o_c = o_b + CONV_WIDTH
    o_u = o_c + CONV_WIDTH
    o_q = o_u + CONV_WIDTH
    o_k = o_q + ATTN_WIDTH
    o_v = o_k + ATTN_WIDTH
    o_f = o_v + ATTN_WIDTH
    h = x
    for i in range(DEPTH):
        xn = rms_norm(h, mix_norm[i])
        z = jnp.einsum('bsd,de->bse', xn, w_in[i])
        gate_b = z[..., o_b:o_c]
        gate_c = z[..., o_c:o_u]
        u = z[..., o_u:o_q]
        y_conv = gate_b * causal_depthwise_conv(gate_c * u, conv_w[i])
        q = z[..., o_q:o_k].reshape(B, S, N_ATTN_HEADS, HEAD_DIM)
        k = z[..., o_k:o_v].reshape(B, S, N_ATTN_HEADS, HEAD_DIM)
        v = z[..., o_v:o_f].reshape(B, S, N_ATTN_HEADS, HEAD_DIM)
        log_f = jax.nn.log_sigmoid(z[..., o_f:].astype(jnp.float32)
                                   + b_f[i].astype(jnp.float32))
        y_attn = forgetting_attention(q, k, v, log_f).reshape(B, S, ATTN_WIDTH)
        y = group_rms_norm(jnp.concatenate([y_conv, y_attn], axis=-1), mix_out_norm[i])
        h = h + jnp.einsum('bse,ed->bsd', y, w_o[i])
        gu = jnp.einsum('bsd,df->bsf', rms_norm(h, ffn_norm[i]), w_gate_up[i])
        g, up = gu[..., :D_FF], gu[..., D_FF:]
        h = h + jnp.einsum('bsf,fd->bsd', jax.nn.silu(g) * up, w_down[i])
        gate = jax.nn.sigmoid(jnp.einsum('bsd,de->bse', rms_norm(h, ple_norm[i]), w_ple_gate[i])
                              + b_ple_gate[i])
        h = h + gate * jnp.einsum('bsp,pd->bsd', p[i], w_ple_proj[i])
    return rms_norm(h, final_norm)
```

```python
import numpy as np
from contextlib import ExitStack
import concourse.bass as bass
import concourse.mybir as mybir
from concourse.bass_utils import run_bass_kernel_spmd

F32 = mybir.dt.float32
BF16 = mybir.dt.bfloat16
AF = mybir.ActivationFunctionType
ALU = mybir.AluOpType

D = 1024
S = 4096
T = 512
DFF = 2816
EPS = 1e-6
NCORES = 8
SEQ_PER_CORE = 2
NEG = -30000.0


class Buf:
    __slots__ = ("w", "r")

    def __init__(self):
        self.w = None
        self.r = {}


class Prog:
    ENG = ("pe", "act", "dve", "pool", "sp")

    def __init__(self):
        self.q = {e: [] for e in self.ENG}
        self.cnt = {e: 0 for e in self.ENG}
        self.dma_cnt = {}

    @staticmethod
    def _deps(reads, writes):
        deps = []
        for b in reads:
            if b.w is not None:
                deps.append(b.w)
        for b in writes:
            if b.w is not None:
                deps.append(b.w)
            deps.extend(b.r.items())
        return deps

    @staticmethod
    def _update(tok, reads, writes):
        for b in writes:
            b.w = tok
            b.r = {}
        for b in reads:
            if not any(b is w for w in writes):
                if b.r.get(tok[0], 0) < tok[1]:
                    b.r[tok[0]] = tok[1]

    def op(self, eng, fn, reads=(), writes=()):
        deps = self._deps(reads, writes)
        self.cnt[eng] += 1
        tok = (eng, self.cnt[eng])
        self.q[eng].append((fn, deps, None))
        self._update(tok, reads, writes)
        return tok

    def dma(self, eng, fn, sem, reads=(), writes=()):
        deps = self._deps(reads, writes)
        self.dma_cnt[sem] = self.dma_cnt.get(sem, 0) + 16
        tok = (sem, self.dma_cnt[sem])
        self.q[eng].append((fn, deps, sem))
        self._update(tok, reads, writes)
        return tok

    def dma_multi(self, eng, fns, sem, reads=(), writes=()):
        deps = self._deps(reads, writes)
        tok = None
        for fn in fns:
            self.dma_cnt[sem] = self.dma_cnt.get(sem, 0) + 16
            tok = (sem, self.dma_cnt[sem])
            self.q[eng].append((fn, deps, sem))
        self._update(tok, reads, writes)
        return tok

    def emit(self, nc, es):
        sems = {}
        for e in self.ENG:
            sems[e] = es.enter_context(nc.semaphore("s_" + e))
        for name in self.dma_cnt:
            sems[name] = es.enter_context(nc.semaphore("d_" + name))
        block = es.enter_context(nc.Block())

        needed = {e: set() for e in self.ENG}
        for engname in self.ENG:
            waited = {}
            for fn, deps, dsem in self.q[engname]:
                for (k, v) in deps:
                    if waited.get(k, 0) < v:
                        waited[k] = v
                        if k in needed:
                            needed[k].add(v)
        rank = {}
        for e in self.ENG:
            rank[e] = {v: i + 1 for i, v in enumerate(sorted(needed[e]))}
        self.n_inc = {e: len(needed[e]) for e in self.ENG}

        def run(engname, eng, final=False):
            waited = {}
            idx = 0
            for fn, deps, dsem in self.q[engname]:
                for (k, v) in deps:
                    if waited.get(k, 0) < v:
                        eng.wait_ge(sems[k], rank[k][v] if k in rank else v)
                        waited[k] = v
                ins = fn(eng)
                if dsem is None:
                    idx += 1
                    if idx in needed[engname]:
                        ins.then_inc(sems[engname], 1)
                else:
                    ins.then_inc(sems[dsem], 16)
            if final:
                for name, v in self.dma_cnt.items():
                    if waited.get(name, 0) < v:
                        eng.wait_ge(sems[name], v)

        @block.tensor
        def _(e):
            run("pe", e)

        @block.scalar
        def _(e):
            run("act", e)

        @block.vector
        def _(e):
            run("dve", e)

        @block.gpsimd
        def _(e):
            run("pool", e)

        @block.sync
        def _(e):
            run("sp", e, final=True)


def block_groups():
    specs = []

    def wspec(kind, **kw):
        d = dict(kind=kind)
        d.update(kw)
        specs.append(d)
        return len(specs) - 1

    d = {}
    d["win"] = {nm: wspec("cols", w="w_in", c0=512 * j) for nm, j in
                (("q", 3), ("k", 4), ("v", 5), ("b", 0), ("c", 1), ("u", 2))}
    d["woc"] = wspec("rows", w="w_o", r0=0, n=4, c0=0, ncol=1024)
    d["woa"] = [wspec("woa", c0=512 * dh) for dh in range(2)]
    d["ffn"] = []
    for (f_lo, f_n) in ((0, 12), (12, 10)):
        gu = [wspec("gu", f0=(f_lo + 2 * j) * 128) for j in range(f_n // 2)]
        dn = []
        n1 = f_n // 2
        for dh in range(2):
            a = wspec("rows", w="w_dn", r0=f_lo * 128, n=n1, c0=512 * dh, ncol=512)
            b = wspec("rows", w="w_dn", r0=(f_lo + n1) * 128, n=f_n - n1, c0=512 * dh, ncol=512)
            dn.append((a, b, n1, f_n - n1))
        d["ffn"].append((f_lo, f_n, gu, dn))
    d["pg"] = [wspec("cols", w="w_pg", c0=512 * dh) for dh in range(2)]
    d["pp"] = wspec("rows", w="w_pp", r0=0, n=2, c0=0, ncol=1024)
    return specs, d


def group_len(g):
    return g["n"] * g["ncol"] if g["kind"] == "rows" else 4096


def pack_weights(ws):
    specs, _ = block_groups()
    out = np.zeros((len(specs), 128, 4096), np.float32)
    for i, g in enumerate(specs):
        k = g["kind"]
        if k == "cols":
            w = ws[g["w"]][:, g["c0"]:g["c0"] + 512]
            out[i] = w.reshape(8, 128, 512).transpose(1, 0, 2).reshape(128, 4096)
        elif k == "gu":
            w = ws["w_gu"]
            f0 = g["f0"]
            both = np.concatenate([w[:, f0:f0 + 256], w[:, DFF + f0:DFF + f0 + 256]], axis=1)
            out[i] = both.reshape(8, 128, 512).transpose(1, 0, 2).reshape(128, 4096)
        elif k == "rows":
            w = ws[g["w"]][g["r0"]:g["r0"] + g["n"] * 128, g["c0"]:g["c0"] + g["ncol"]]
            out[i, :, 0:g["n"] * g["ncol"]] = w.reshape(g["n"], 128, g["ncol"]).transpose(1, 0, 2).reshape(128, -1)
        elif k == "woa":
            w = ws["w_o"][512:1024, g["c0"]:g["c0"] + 512]
            out[i, 0:64, :] = w.reshape(8, 64, 512).transpose(1, 0, 2).reshape(64, 4096)
    return out


def build(nc, nblk=8, nseq=SEQ_PER_CORE):
    P = Prog()
    es = ExitStack()

    def dram(name, shape, kind="ExternalInput"):
        return nc.dram_tensor(name, list(shape), F32, kind=kind).ap()

    x_d = dram("x", [SEQ_PER_CORE, S, D])
    p_d = dram("p", [SEQ_PER_CORE, S, 256])
    bspecs, bplan = block_groups()
    NG = len(bspecs)
    wpack_d = dram("wpack", [NG, 128, 4096])
    wf_d = dram("wf", [128, 64])
    gpc_d = dram("gpc", [128, 24])
    gconv_d = dram("gconv", [128, 4])
    gattn_d = dram("gattn", [64, 8])
    cw_d = dram("cw", [128, 12])
    bfbc_d = dram("bfbc", [128, 8])
    gfin_d = dram("gfin", [128, D])
    bple_d = dram("bple", [128, D])
    ident_d = dram("ident", [128, 128])
    tri_d = dram("tri", [128, 128])
    ones_d = dram("ones", [128, 128])
    maskb_d = dram("maskb", [128, 128])
    bdiag_d = dram("bdiag", [128, 128])
    wn_d = dram("wn", [128, 128])
    onesrow_d = dram("onesrow", [128, 128])
    sel_d = dram("sel", [128, 1024])
    y_d = dram("y", [SEQ_PER_CORE, S, D], kind="ExternalOutput")

    def sb(name, shape, dt):
        return es.enter_context(nc.sbuf_tensor(name, list(shape), dt))

    def psum(name):
        return es.enter_context(nc.psum_tensor(name, [128, 512], F32))

    identb = sb("identb", [128, 128], BF16)
    identf = sb("identf", [128, 128], F32)
    trif = sb("trif", [128, 128], F32)
    onesf = sb("onesf", [128, 128], F32)
    maskb = sb("maskb_s", [128, 128], BF16)
    bdiag = sb("bdiag_s", [128, 128], BF16)
    wn = sb("wn_s", [128, 128], BF16)
    sel = sb("sel_s", [128, 1024], BF16)
    bple = sb("bple_s", [128, D], BF16)
    onesrow = sb("onesrow_s", [128, 128], BF16)
    wf = sb("wf_s", [128, 8, 8], BF16)
    gpc = sb("gpc_s", [128, 24], F32)
    gconv = sb("gconv_s", [128, 4], F32)
    gattn = sb("gattn_s", [64, 8], F32)
    cw = sb("cw_s", [128, 12], F32)
    bfbc = sb("bfbc_s", [128, 8], F32)
    gfin = sb("gfin_s", [128, D], F32)
    Bconst = Buf()

    def cdma(eng, out, in_):
        P.dma(eng, lambda e, out=out, in_=in_: e.dma_start(out=out, in_=in_), "c")

    cdma("sp", identf[:], ident_d[:, :])
    cdma("sp", trif[:], tri_d[:, :])
    cdma("sp", onesf[:], ones_d[:, :])
    cdma("sp", gpc[:], gpc_d[:, :])
    cdma("sp", gconv[:], gconv_d[:, :])
    cdma("sp", gattn[:], gattn_d[:, :])
    cdma("sp", cw[:], cw_d[:, :])
    cdma("sp", bfbc[:], bfbc_d[:, :])
    cdma("sp", gfin[:], gfin_d[:, :])
    cdma("pool", identb[:], ident_d[:, :])
    cdma("pool", maskb[:], maskb_d[:, :])
    cdma("pool", bdiag[:], bdiag_d[:, :])
    cdma("pool", wn[:], wn_d[:, :])
    cdma("pool", sel[:], sel_d[:, :])
    cdma("pool", bple[:], bple_d[:, :])
    cdma("pool", onesrow[:], onesrow_d[:, :])
    cdma("pool", wf[:], wf_d[:, :].rearrange("p (a b) -> p a b", a=8))
    Bconst.w = ("c", P.dma_cnt["c"])

    KT = sb("KT", [128, 4, S], BF16)
    VA = sb("VA", [128, 32, 8, 65], BF16)
    cK = sb("cK", [128, 32, 8], F32)
    carry = sb("carry", [128, 8], F32)
    ccar = sb("ccar", [128, 4, 2], F32)
    BKT = [Buf() for _ in range(4)]
    BVA = Buf()
    BcK = Buf()
    Bcarry = Buf()
    Bccar = Buf()

    hT = [[sb(f"h{b}_{t}", [128, D], F32) for t in range(4)] for b in range(2)]
    Bh = [[Buf() for _ in range(4)] for _ in range(2)]
    xnT = sb("xnT", [128, 8, T], BF16)
    BxnT = Buf()
    xnbf = [sb(f"xnbf{i}", [128, D], BF16) for i in range(2)]
    Bxnbf = [Buf(), Buf()]
    junk = sb("junk", [128, D], BF16)
    Bjunk = Buf()
    stat = sb("stat", [128, 16], F32)
    Bstat = [Buf() for _ in range(4)]
    fst = sb("fst", [128, 32], F32)
    Bfst = Buf()
    QT = sb("QT", [128, 8, T], BF16)
    BQT = [Buf() for _ in range(8)]
    crow = sb("crow", [128, T], BF16)
    Bcrow = Buf()
    NP = 4
    Pb = [sb(f"P{i}", [128, T], BF16) for i in range(NP)]
    BP = [Buf() for _ in range(NP)]
    yT = sb("yT", [128, 4, T], BF16)
    ByT = [Buf() for _ in range(4)]
    yTa = sb("yTa", [128, 8, T], BF16)
    ByTa = [Buf() for _ in range(8)]
    cu = sb("cu", [128, T + 2], F32)
    Bcu = Buf()
    acc = sb("acc", [128, T], F32)
    Bacc = Buf()
    rs = sb("rs", [128, T], F32)
    Brs = Buf()
    rsa, Brsa = rs, Brs
    csb, Bcsb = rs, Brs
    sqb = sb("sqb", [128, T], BF16)
    Bsqb = Buf()
    osq, Bosq = sqb, Bsqb
    NACT = 12
    actT = sb("actT", [128, NACT, T], BF16)
    BactT = [Buf() for _ in range(NACT)]
    sg = [sqb, sb("sg1", [128, T], BF16)]
    Bsg = [Bsqb, Buf()]
    pin = [sb(f"pin{i}", [128, 256], F32) for i in range(2)]
    Bpin = [Buf(), Buf()]
    pbf = sb("pbf", [128, 4, 256], BF16)
    Bpbf = [Buf() for _ in range(4)]
    pT = sb("pT", [128, 2, T], BF16)
    BpT = Buf()
    NSLOT = 4
    wslot = [sb(f"wslot{i}", [128, 4096], BF16) for i in range(NSLOT)]
    Bslot = [Buf() for _ in range(NSLOT)]

    PS = [psum(f"ps{i}") for i in range(8)]
    Bps = [Buf() for _ in range(8)]
    ring = {"mm": 0, "tt": 0, "s": 0, "o": 0, "g": 0}

    def bank(kind="mm"):
        if kind == "mm":
            b = ring["mm"] % 6
        elif kind == "tt":
            b = 6 + ring["tt"] % 2
        elif kind == "s":
            b = ring["s"] % 4
        elif kind == "o":
            b = 4 + ring["o"] % 2
        else:
            b = 6 + ring["g"] % 2
        ring[kind] += 1
        return b

    P.op("dve", lambda e: e.memset(VA[:, :, :, 64:65], 1.0), writes=[BVA])
    P.op("dve", lambda e: e.memset(QT[:], 0.0), writes=BQT)
    P.op("dve", lambda e: e.memset(crow[:], 0.0), writes=[Bcrow])
    P.op("dve", lambda e: e.memset(osq[:], 0.0), writes=[Bosq])
    P.op("dve", lambda e: e.memset(rs[:], 1.0), writes=[Brs])
    P.op("dve", lambda e: e.memset(yTa[:], 0.0), writes=ByTa)

    wstate = {"issued": 0}
    NGROUPS = NG * nseq * nblk
    Bchain = Buf()
    Bchain.w = Bconst.w

    def issue_group(i):
        g = bspecs[i % NG]
        s_ = i % NSLOT
        n = group_len(g)
        o = wslot[s_][:, 0:n]
        i_ = wpack_d[i % NG, :, 0:n]
        wr = [Bslot[s_], Bchain] if i < NSLOT else [Bslot[s_]]
        P.dma("pool", lambda e, o=o, i_=i_: e.dma_start(out=o, in_=i_), f"w{s_}", writes=wr)

    released = set()

    def wpump():
        while wstate["issued"] < NGROUPS:
            j = wstate["issued"]
            if j >= NSLOT and (j - NSLOT) not in released:
                break
            issue_group(j)
            wstate["issued"] += 1

    def wget(i):
        wpump()
        assert wstate["issued"] > i, (i, wstate["issued"])
        return wslot[i % NSLOT], Bslot[i % NSLOT]

    def wrel(*idx):
        for i in idx:
            released.add(i)
        wpump()

    def shift(v, off):
        if isinstance(v, dict):
            return {k: shift(x, off) for k, x in v.items()}
        if isinstance(v, list):
            return [shift(x, off) for x in v]
        return v

    plan = []
    for gb in range(nseq * nblk):
        off = gb * NG
        d = {}
        d["win"] = {k: v + off for k, v in bplan["win"].items()}
        d["woc"] = bplan["woc"] + off
        d["woa"] = [v + off for v in bplan["woa"]]
        d["ffn"] = [(f_lo, f_n, [v + off for v in gu], [(a + off, b + off, na, nb) for (a, b, na, nb) in dn])
                    for (f_lo, f_n, gu, dn) in bplan["ffn"]]
        d["pg"] = [v + off for v in bplan["pg"]]
        d["pp"] = bplan["pp"] + off
        plan.append(d)

    def load_x(sq, blk, hb):
        for tt in range(4):
            r0 = blk * T + tt * 128
            P.dma("sp", lambda e, o=hT[hb][tt][:], i_=x_d[sq, r0:r0 + 128, :]: e.dma_start(out=o, in_=i_),
                  f"h{hb}_{tt}", writes=[Bh[hb][tt]])

    def rstd_all(hb):
        for tt in range(4):
            h = hT[hb][tt]
            P.op("act", lambda e, h=h, tt=tt: e.activation(out=junk[:], in_=h[:], func=AF.Square,
                                                            accum_out=stat[:, tt:tt + 1]),
                 reads=[Bh[hb][tt]], writes=[Bjunk, Bstat[tt]])
        P.op("act", lambda e: e.activation(out=stat[:, 4:8], in_=stat[:, 0:4], func=AF.Ln,
                                           scale=1.0 / D, bias=EPS), reads=[], writes=Bstat)
        P.op("act", lambda e: e.activation(out=stat[:, 8:12], in_=stat[:, 4:8], func=AF.Exp, scale=-0.5),
             reads=[], writes=Bstat)

    def rstd_tiles(hb):
        for tt in range(4):
            h = hT[hb][tt]
            P.op("act", lambda e, h=h, tt=tt: e.activation(out=junk[:], in_=h[:], func=AF.Square,
                                                            accum_out=stat[:, tt:tt + 1]),
                 reads=[Bh[hb][tt]], writes=[Bjunk, Bstat[tt]])
            P.op("act", lambda e, tt=tt: e.activation(out=stat[:, 4 + tt:5 + tt], in_=stat[:, tt:tt + 1], func=AF.Ln,
                                                       scale=1.0 / D, bias=EPS), reads=[], writes=[Bstat[tt]])
            P.op("act", lambda e, tt=tt: e.activation(out=stat[:, 8 + tt:9 + tt], in_=stat[:, 4 + tt:5 + tt],
                                                       func=AF.Exp, scale=-0.5), reads=[], writes=[Bstat[tt]])

    def norm_to_xnT(hb, goff):
        rstd_tiles(hb)
        norm_tail(hb, goff)

    def norm_tail(hb, goff, pre_only=False, skip_pre=False):
        def xn(tt):
            h = hT[hb][tt]
            xb = xnbf[tt % 2]
            P.op("dve", lambda e, h=h, tt=tt, xb=xb: e.tensor_scalar(out=xb[:], in0=h[:],
                                                                      scalar1=stat[:, 8 + tt:9 + tt],
                                                                      scalar2=None, op0=ALU.mult),
                 reads=[Bh[hb][tt], Bstat[tt]], writes=[Bxnbf[tt % 2]])
        if not skip_pre:
            xn(0)
            xn(1)
        if pre_only:
            return
        for tt in range(4):
            xb = xnbf[tt % 2]
            tb = bank("tt")
            pbv = PS[tb][:].bitcast(BF16)

            def tr(e, xb=xb, pbv=pbv):
                ins = None
                for kc in range(8):
                    ins = e.transpose(out=pbv[:, kc * 128:(kc + 1) * 128], in_=xb[:, kc * 128:(kc + 1) * 128],
                                      identity=identb[:])
                return ins
            P.op("pe", tr, reads=[Bxnbf[tt % 2], Bconst], writes=[Bps[tb]])
            if tt + 2 < 4:
                xn(tt + 2)
            gb = gpc[:, goff:goff + 8].unsqueeze(2).to_broadcast([128, 8, 128])
            P.op("dve", lambda e, pbv=pbv, gb=gb, tt=tt: e.tensor_tensor(
                out=xnT[:, :, tt * 128:(tt + 1) * 128], in0=pbv.rearrange("p (k t) -> p k t", k=8), in1=gb,
                op=ALU.mult), reads=[Bps[tb], Bconst], writes=[BxnT])

    def mm_fm(b, wv, Bw, c0):
        def f(e):
            ins = None
            for kc in range(8):
                ins = e.matmul(PS[b][:], lhsT=wv[:, kc, c0:c0 + 128], rhs=xnT[:, kc, :],
                               start=(kc == 0), stop=(kc == 7))
            return ins
        P.op("pe", f, reads=[Bw, BxnT], writes=[Bps[b]])

    def w8(slot):
        return slot[:, 0:4096].rearrange("p (a b) -> p a b", a=8)

    gblk = 0
    load_x(0, 0, 0)
    for sq in range(nseq):
        for blk in range(nblk):
            hb = gblk % 2
            pl = plan[gblk]
            q0 = blk * T
            nxt = gblk + 1
            if nxt < nseq * nblk:
                load_x(nxt // nblk, nxt % nblk, nxt % 2)
            for tt in range(4):
                r0 = q0 + tt * 128
                P.dma("sp", lambda e, o=pin[tt % 2][:], i_=p_d[sq, r0:r0 + 128, :]: e.dma_start(out=o, in_=i_),
                      f"pin{tt % 2}", writes=[Bpin[tt % 2]])
                P.op("dve", lambda e, tt=tt: e.tensor_copy(out=pbf[:, tt, :], in_=pin[tt % 2][:]),
                     reads=[Bpin[tt % 2]], writes=[Bpbf[tt]])
            if blk == 0:
                P.op("dve", lambda e: e.memset(carry[:], 0.0), writes=[Bcarry])
                P.op("dve", lambda e: e.memset(ccar[:], 0.0), writes=[Bccar])

            if gblk == 0:
                norm_to_xnT(hb, 0)

            sQ, BQ = wget(pl["win"]["q"])
            for hp in range(4):
                b = bank()
                mm_fm(b, w8(sQ), BQ, hp * 128)
                P.op("act", lambda e, b=b, hp=hp: e.activation(out=QT[0:64, 2 * hp, :], in_=PS[b][0:64, :],
                                                                func=AF.Copy, scale=0.125),
                     reads=[Bps[b]], writes=[BQT[2 * hp]])
                P.op("act", lambda e, b=b, hp=hp: e.activation(out=QT[64:128, 2 * hp + 1, :], in_=PS[b][64:128, :],
                                                                func=AF.Copy, scale=0.125),
                     reads=[Bps[b]], writes=[BQT[2 * hp + 1]])
            wrel(pl["win"]["q"])
            sK, BK = wget(pl["win"]["k"])
            for hp in range(4):
                b = bank()
                mm_fm(b, w8(sK), BK, hp * 128)
                P.op("dve", lambda e, b=b, hp=hp, q0=q0: e.tensor_copy(out=KT[:, hp, q0:q0 + T], in_=PS[b][:]),
                     reads=[Bps[b]], writes=[BKT[hp]])
            wrel(pl["win"]["k"])
            sV, BV = wget(pl["win"]["v"])
            wv = w8(sV)
            def v_mm(tt):
                bv, bz = bank(), bank()

                def fv(e, tt=tt, bv=bv, bz=bz, wv=wv):
                    ins = None
                    for kc in range(8):
                        e.matmul(PS[bv][:], lhsT=xnT[:, kc, tt * 128:(tt + 1) * 128], rhs=wv[:, kc, :],
                                 start=(kc == 0), stop=(kc == 7))
                    for kc in range(8):
                        ins = e.matmul(PS[bz][:, 0:8], lhsT=xnT[:, kc, tt * 128:(tt + 1) * 128], rhs=wf[:, kc, :],
                                       start=(kc == 0), stop=(kc == 7))
                    return ins
                P.op("pe", fv, reads=[BxnT, BV, Bconst], writes=[Bps[bv], Bps[bz]])
                return bv, bz

            def v_post1(tt, bv, bz):
                kt = blk * 4 + tt
                P.op("dve", lambda e, kt=kt, bv=bv: e.tensor_copy(
                    out=VA[:, kt, :, 0:64], in_=PS[bv][:].rearrange("p (h d) -> p h d", h=8)),
                    reads=[Bps[bv]], writes=[BVA])
                P.op("dve", lambda e, bz=bz: e.tensor_tensor(out=fst[:, 0:8], in0=PS[bz][:, 0:8], in1=bfbc[:],
                                                              op=ALU.add),
                     reads=[Bps[bz], Bconst], writes=[Bfst])
                P.op("act", lambda e: e.activation(out=fst[:, 8:16], in_=fst[:, 0:8], func=AF.Exp, scale=-1.0),
                     reads=[], writes=[Bfst])
                P.op("act", lambda e: e.activation(out=fst[:, 16:24], in_=fst[:, 8:16], func=AF.Ln, bias=1.0),
                     reads=[], writes=[Bfst])

            def v_post2(tt):
                kt = blk * 4 + tt
                bc = bank()

                def fc_(e, bc=bc):
                    e.matmul(PS[bc][:, 0:8], lhsT=trif[:], rhs=fst[:, 16:24], start=True, stop=True)
                    return e.matmul(PS[bc][:, 8:16], lhsT=onesf[:], rhs=fst[:, 16:24], start=True, stop=True)
                P.op("pe", fc_, reads=[Bfst, Bconst], writes=[Bps[bc]])
                P.op("dve", lambda e, bc=bc, kt=kt: e.tensor_tensor(out=cK[:, kt, :], in0=PS[bc][:, 0:8],
                                                                    in1=carry[:], op=ALU.add),
                     reads=[Bps[bc], Bcarry], writes=[BcK])
                P.op("dve", lambda e, bc=bc: e.tensor_tensor(out=carry[:], in0=PS[bc][:, 8:16], in1=carry[:],
                                                              op=ALU.add),
                     reads=[Bps[bc]], writes=[Bcarry])

            vb = v_mm(0)
            for tt in range(4):
                v_post1(tt, *vb)
                if tt + 1 < 4:
                    vb = v_mm(tt + 1)
                v_post2(tt)
            wrel(pl["win"]["v"])
            bx = bank()

            def ftr(e, bx=bx, blk=blk):
                ins = None
                for tt in range(4):
                    ins = e.transpose(out=PS[bx][0:8, tt * 128:(tt + 1) * 128], in_=cK[:, blk * 4 + tt, :],
                                      identity=identf[:])
                return ins
            P.op("pe", ftr, reads=[BcK, Bconst], writes=[Bps[bx]])
            P.op("act", lambda e, bx=bx: e.activation(out=crow[0:8, :], in_=PS[bx][0:8, :], func=AF.Copy, scale=-1.0),
                 reads=[Bps[bx]], writes=[Bcrow])

            gb_, gc_, gu_ = pl["win"]["b"], pl["win"]["c"], pl["win"]["u"]
            sB, BB = wget(gb_)
            sC, BC = wget(gc_)
            sU, BU = wget(gu_)
            def conv_mm(c):
                b_b, b_c, b_u = bank(), bank(), bank()
                mm_fm(b_b, w8(sB), BB, c * 128)
                mm_fm(b_c, w8(sC), BC, c * 128)
                mm_fm(b_u, w8(sU), BU, c * 128)
                return b_b, b_c, b_u

            def conv_chain(c, banks):
                b_b, b_c, b_u = banks
                P.op("act", lambda e, b_c=b_c: e.activation(out=csb[:], in_=PS[b_c][:], func=AF.Copy),
                     reads=[Bps[b_c]], writes=[Bcsb])
                P.op("dve", lambda e, c=c: e.tensor_copy(out=cu[:, 0:2], in_=ccar[:, c, :]),
                     reads=[Bccar], writes=[Bcu])
                P.op("dve", lambda e, b_u=b_u: e.tensor_tensor(out=cu[:, 2:T + 2], in0=csb[:], in1=PS[b_u][:],
                                                                op=ALU.mult),
                     reads=[Bcsb, Bps[b_u]], writes=[Bcu])
                P.op("dve", lambda e, c=c: e.tensor_copy(out=ccar[:, c, :], in_=cu[:, T:T + 2]),
                     reads=[Bcu], writes=[Bccar])
                P.op("dve", lambda e, c=c: e.tensor_scalar(out=acc[:], in0=cu[:, 2:T + 2],
                                                           scalar1=cw[:, c * 3 + 2:c * 3 + 3], scalar2=None,
                                                           op0=ALU.mult),
                     reads=[Bcu, Bconst], writes=[Bacc])
                P.op("dve", lambda e, c=c: e.scalar_tensor_tensor(out=acc[:], in0=cu[:, 1:T + 1],
                                                                  scalar=cw[:, c * 3 + 1:c * 3 + 2], in1=acc[:],
                                                                  op0=ALU.mult, op1=ALU.add),
                     reads=[Bcu, Bconst], writes=[Bacc])
                P.op("dve", lambda e, c=c: e.scalar_tensor_tensor(out=acc[:], in0=cu[:, 0:T],
                                                                  scalar=cw[:, c * 3:c * 3 + 1], in1=acc[:],
                                                                  op0=ALU.mult, op1=ALU.add),
                     reads=[Bcu, Bconst], writes=[Bacc])
                P.op("dve", lambda e, b_b=b_b: e.tensor_tensor(out=acc[:], in0=acc[:], in1=PS[b_b][:], op=ALU.mult),
                     reads=[Bps[b_b]], writes=[Bacc])
                P.op("act", lambda e: e.activation(out=sqb[:], in_=acc[:], func=AF.Square),
                     reads=[Bacc], writes=[Bsqb])

            def conv_fin(c):
                gbk = bank("g")
                P.op("pe", lambda e, gbk=gbk: e.matmul(PS[gbk][:], lhsT=bdiag[:], rhs=sqb[:], start=True, stop=True),
                     reads=[Bsqb, Bconst], writes=[Bps[gbk]])
                P.op("act", lambda e, gbk=gbk: e.activation(out=rs[:], in_=PS[gbk][:], func=AF.Ln, bias=EPS),
                     reads=[Bps[gbk]], writes=[Brs])
                P.op("act", lambda e: e.activation(out=rs[:], in_=rs[:], func=AF.Exp, scale=-0.5),
                     reads=[], writes=[Brs])
                P.op("dve", lambda e, c=c: e.scalar_tensor_tensor(out=yT[:, c, :], in0=acc[:],
                                                                  scalar=gconv[:, c:c + 1], in1=rs[:],
                                                                  op0=ALU.mult, op1=ALU.mult),
                     reads=[Bacc, Brs, Bconst], writes=[ByT[c]])

            cbanks = conv_mm(0)
            for c in range(3):
                conv_chain(c, cbanks)
                cbanks = conv_mm(c + 1)
                conv_fin(c)
            wrel(gb_, gc_, gu_)
            conv_chain(3, cbanks)
            conv_fin3 = [lambda: conv_fin(3)]
            units = []
            for h in range(8):
                lst = [(kt, 0, False) for kt in range(blk * 4)] + [(blk * 4 + j, 128 * j, True) for j in range(4)]
                for ui, (kt, c0, dg) in enumerate(lst):
                    units.append((h, kt, c0, dg, ui == 0, ui == len(lst) - 1))
            LOOK = 2
            obank = {}
            slots = {}
            deferred = []

            def post_pe(h, ob):
                gbk = bank("g")
                P.op("pe", lambda e, gbk=gbk: e.matmul(PS[gbk][:, :], lhsT=wn[:, :], rhs=osq[:, :],
                                                         start=True, stop=True),
                     reads=[Bosq, Bconst], writes=[Bps[gbk]])
                P.op("act", lambda e, gbk=gbk: e.activation(out=rsa[0:64, :], in_=PS[gbk][0:64, :], func=AF.Ln),
                     reads=[Bps[gbk]], writes=[Brsa])
                P.op("act", lambda e: e.activation(out=rsa[0:64, :], in_=rsa[0:64, :], func=AF.Exp, scale=-0.5),
                     reads=[], writes=[Brsa])
                P.op("dve", lambda e, h=h, ob=ob: e.scalar_tensor_tensor(
                    out=yTa[0:64, h, :], in0=PS[ob][0:64, :], scalar=gattn[:, h:h + 1], in1=rsa[0:64, :],
                    op0=ALU.mult, op1=ALU.mult), reads=[Bps[ob], Brsa, Bconst], writes=[ByTa[h]])

            for i in range(len(units) + LOOK):
                if i < len(units):
                    h, kt, c0, dg, first, last = units[i]
                    hp, s_ = h // 2, h % 2
                    rows = slice(64 * s_, 64 * s_ + 64)
                    sbk = bank("s")
                    pi = i % NP
                    slots[i] = pi

                    def fqk(e, sbk=sbk, hp=hp, rows=rows, kt=kt, c0=c0, dg=dg, h=h):
                        e.matmul(PS[sbk][:, c0:T], lhsT=KT[:, hp, kt * 128:(kt + 1) * 128],
                                 rhs=QT[:, h, c0:T], start=True, stop=False)
                        ins = e.matmul(PS[sbk][:, c0:T], lhsT=sel[:, h * 128:(h + 1) * 128], rhs=crow[:, c0:T],
                                       start=False, stop=(not dg))
                        if dg:
                            ins = e.matmul(PS[sbk][:, c0:c0 + 128], lhsT=identb[:], rhs=maskb[:],
                                           start=False, stop=True)
                        return ins
                    P.op("pe", fqk, reads=[BKT[hp], BQT[h], Bcrow, Bconst], writes=[Bps[sbk]])
                    P.op("act", lambda e, sbk=sbk, pi=pi, c0=c0, kt=kt, h=h: e.activation(
                        out=Pb[pi][:, c0:T], in_=PS[sbk][:, c0:T], func=AF.Exp, bias=cK[:, kt, h:h + 1], scale=1.0),
                        reads=[Bps[sbk], BcK], writes=[BP[pi]])
                j = i - LOOK
                if j >= 0:
                    h, kt, c0, dg, first, last = units[j]
                    if first:
                        obank[h] = bank("o")
                    ob = obank[h]
                    pi = slots[j]
                    P.op("pe", lambda e, ob=ob, kt=kt, h=h, pi=pi, c0=c0, first=first, last=last: e.matmul(
                        PS[ob][0:65, c0:T], lhsT=VA[:, kt, h, :], rhs=Pb[pi][:, c0:T], start=first, stop=last),
                        reads=[BP[pi], BVA], writes=[Bps[ob]])
                    if last:
                        P.op("act", lambda e, ob=ob: e.activation(out=osq[0:65, :], in_=PS[ob][0:65, :],
                                                                   func=AF.Square),
                             reads=[Bps[ob]], writes=[Bosq])
                        deferred.append((i + 2, (lambda h=h, ob=ob: post_pe(h, ob))))
                if i == 2:
                    conv_fin3.pop()()
                while deferred and deferred[0][0] <= i:
                    deferred.pop(0)[1]()
            while deferred:
                deferred.pop(0)[1]()

            sWc, BWc = wget(pl["woc"])
            wcv = sWc[:, 0:4096].rearrange("p (a b) -> p a b", a=4)
            woa_ = [wget(pl["woa"][dh]) for dh in range(2)]
            for tt in range(4):
                for dh in range(2):
                    sWa, BWa = woa_[dh]
                    wav = sWa[:, 0:4096].rearrange("p (a b) -> p a b", a=8)
                    b = bank()

                    def fo(e, b=b, tt=tt, dh=dh, wav=wav, wcv=wcv):
                        for c in range(4):
                            e.matmul(PS[b][:], lhsT=yT[:, c, tt * 128:(tt + 1) * 128],
                                     rhs=wcv[:, c, dh * 512:(dh + 1) * 512], start=(c == 0), stop=False)
                        ins = None
                        for h in range(8):
                            ins = e.matmul(PS[b][:], lhsT=yTa[:, h, tt * 128:(tt + 1) * 128], rhs=wav[:, h, :],
                                           start=False, stop=(h == 7))
                        return ins
                    P.op("pe", fo, reads=ByT + ByTa + [BWc, BWa], writes=[Bps[b]])
                    hh = hT[hb][tt]
                    P.op("dve", lambda e, b=b, hh=hh, dh=dh: e.tensor_tensor(
                        out=hh[:, dh * 512:(dh + 1) * 512], in0=hh[:, dh * 512:(dh + 1) * 512], in1=PS[b][:],
                        op=ALU.add), reads=[Bps[b]], writes=[Bh[hb][tt]])
            wrel(pl["woa"][0], pl["woa"][1], pl["woc"])

            norm_to_xnT(hb, 8)
            for (f_lo, f_n, gu, dn) in pl["ffn"]:
                for j in range(f_n):
                    sG, BG = wget(gu[j // 2])
                    gv = w8(sG)
                    bg, bu = bank(), bank()
                    mm_fm(bg, gv, BG, (j % 2) * 128)
                    mm_fm(bu, gv, BG, 256 + (j % 2) * 128)
                    si = j % 2
                    P.op("act", lambda e, bg=bg, si=si: e.activation(out=sg[si][:], in_=PS[bg][:], func=AF.Silu),
                         reads=[Bps[bg]], writes=[Bsg[si]])
                    P.op("dve", lambda e, bu=bu, si=si, j=j: e.tensor_tensor(out=actT[:, j, :], in0=sg[si][:],
                                                                              in1=PS[bu][:], op=ALU.mult),
                         reads=[Bsg[si], Bps[bu]], writes=[BactT[j]])
                    if j % 2 == 1:
                        wrel(gu[j // 2])
                for dh in range(2):
                    ga, gb2, na, nb = dn[dh]
                    sA, BA = wget(ga)
                    sB2, BB2 = wget(gb2)
                    av = sA[:, 0:na * 512].rearrange("p (a b) -> p a b", a=na)
                    bv_ = sB2[:, 0:nb * 512].rearrange("p (a b) -> p a b", a=nb)
                    for tt in range(4):
                        b = bank()

                        def fd(e, b=b, tt=tt, av=av, bv_=bv_, na=na, nb=nb):
                            ins = None
                            for j in range(na + nb):
                                rhs = av[:, j, :] if j < na else bv_[:, j - na, :]
                                ins = e.matmul(PS[b][:], lhsT=actT[:, j, tt * 128:(tt + 1) * 128], rhs=rhs,
                                               start=(j == 0), stop=(j == na + nb - 1))
                            return ins
                        P.op("pe", fd, reads=BactT[0:f_n] + [BA, BB2], writes=[Bps[b]])
                        hh = hT[hb][tt]
                        P.op("dve", lambda e, b=b, hh=hh, dh=dh: e.tensor_tensor(
                            out=hh[:, dh * 512:(dh + 1) * 512], in0=hh[:, dh * 512:(dh + 1) * 512], in1=PS[b][:],
                            op=ALU.add), reads=[Bps[b]], writes=[Bh[hb][tt]])
                    wrel(ga, gb2)

            norm_to_xnT(hb, 16)
            tb = bank("tt")
            pbv = PS[tb][:].bitcast(BF16)

            def ftp(e, pbv=pbv):
                ins = None
                for pc in range(2):
                    for tt in range(4):
                        o0 = (pc * 4 + tt) * 128
                        ins = e.transpose(out=pbv[:, o0:o0 + 128], in_=pbf[:, tt, pc * 128:(pc + 1) * 128],
                                          identity=identb[:])
                return ins
            P.op("pe", ftp, reads=Bpbf + [Bconst], writes=[Bps[tb]])
            P.op("dve", lambda e, pbv=pbv: e.tensor_copy(out=pT[:, :, :], in_=pbv.rearrange("p (c t) -> p c t", c=2)),
                 reads=[Bps[tb]], writes=[BpT])
            if nxt < nseq * nblk:
                rstd_all(nxt % 2)
                norm_tail(nxt % 2, 0, pre_only=True)
            sPP, BPP = None, None
            for dh in range(2):
                sPG, BPG = wget(pl["pg"][dh])
                if dh == 0:
                    sPP, BPP = wget(pl["pp"])
                pgv = w8(sPG)
                ppv = sPP[:, 0:2048].rearrange("p (a b) -> p a b", a=2)
                for tt in range(4):
                    bgt, bpp = bank(), bank()

                    def fg(e, bgt=bgt, bpp=bpp, tt=tt, dh=dh, pgv=pgv, ppv=ppv):
                        for kc in range(8):
                            e.matmul(PS[bgt][:], lhsT=xnT[:, kc, tt * 128:(tt + 1) * 128], rhs=pgv[:, kc, :],
                                     start=(kc == 0), stop=False)
                        e.matmul(PS[bgt][:], lhsT=onesrow[:, :], rhs=bple[:, dh * 512:(dh + 1) * 512],
                                 start=False, stop=True)
                        ins = None
                        for pc in range(2):
                            ins = e.matmul(PS[bpp][:], lhsT=pT[:, pc, tt * 128:(tt + 1) * 128],
                                           rhs=ppv[:, pc, dh * 512:(dh + 1) * 512], start=(pc == 0), stop=(pc == 1))
                        return ins
                    P.op("pe", fg, reads=[BxnT, BpT, BPG, BPP, Bconst], writes=[Bps[bgt], Bps[bpp]])
                    P.op("act", lambda e, bgt=bgt: e.activation(out=acc[:], in_=PS[bgt][:], func=AF.Sigmoid),
                         reads=[Bps[bgt]], writes=[Bacc])
                    P.op("dve", lambda e, bpp=bpp: e.tensor_tensor(out=acc[:], in0=acc[:], in1=PS[bpp][:],
                                                                    op=ALU.mult),
                         reads=[Bps[bpp]], writes=[Bacc])
                    hh = hT[hb][tt]
                    P.op("dve", lambda e, hh=hh, dh=dh: e.tensor_tensor(
                        out=hh[:, dh * 512:(dh + 1) * 512], in0=hh[:, dh * 512:(dh + 1) * 512], in1=acc[:],
                        op=ALU.add), reads=[Bacc], writes=[Bh[hb][tt]])
                wrel(pl["pg"][dh])
            wrel(pl["pp"])

            if nxt < nseq * nblk:
                norm_tail(nxt % 2, 0, skip_pre=True)

            rstd_all(hb)
            for tt in range(4):
                h = hT[hb][tt]
                P.op("dve", lambda e, h=h, tt=tt: e.scalar_tensor_tensor(out=h[:], in0=h[:],
                                                                          scalar=stat[:, 8 + tt:9 + tt],
                                                                          in1=gfin[:], op0=ALU.mult, op1=ALU.mult),
                     reads=[Bstat[tt], Bconst], writes=[Bh[hb][tt]])
                r0 = q0 + tt * 128
                P.dma("sp", lambda e, h=h, o=y_d[sq, r0:r0 + 128, :]: e.dma_start(out=o, in_=h[:]),
                      f"h{hb}_{tt}", reads=[Bh[hb][tt]])
            gblk += 1

    P.emit(nc, es)
    P.sbuf_left = nc.sbuf_bytes_remaining
    es.close()
    return P


def _host_consts():
    c = {}
    c["ident"] = np.eye(128, dtype=np.float32)
    j = np.arange(128)
    c["tri"] = (j[:, None] <= j[None, :]).astype(np.float32)
    c["ones"] = np.ones((128, 128), np.float32)
    c["maskb"] = np.where(j[:, None] <= j[None, :], 0.0, NEG).astype(np.float32)
    c["bdiag"] = ((j[:, None] // 64) == (j[None, :] // 64)).astype(np.float32) / 64.0
    wn = np.zeros((128, 128), np.float32)
    wn[0:64, 0:64] = 1.0 / 64.0
    wn[64, 0:64] = EPS
    c["wn"] = wn
    sel = np.zeros((128, 8, 128), np.float32)
    for h in range(8):
        sel[h, h, :] = 1.0
    c["sel"] = sel.reshape(128, 1024)
    orow = np.zeros((128, 128), np.float32)
    orow[0, :] = 1.0
    c["onesrow"] = orow
    return c


_NC_CACHE = {}


def kernel(x, p, mix_norm, w_in, b_f, conv_w, mix_out_norm, w_o, ffn_norm, w_gate_up, w_down,
           ple_norm, w_ple_gate, b_ple_gate, w_ple_proj, final_norm, _nblk=8, _trace=False):
    f = lambda a: np.ascontiguousarray(np.asarray(a, dtype=np.float32))
    x = f(x); p = f(p)
    ws = {"w_in": f(w_in[0]), "w_o": f(w_o[0]), "w_gu": f(w_gate_up[0]), "w_dn": f(w_down[0]),
          "w_pg": f(w_ple_gate[0]), "w_pp": f(w_ple_proj[0])}
    shared = {"wpack": pack_weights(ws),
              "wf": f(ws["w_in"][:, 3072:3080].reshape(8, 128, 8).transpose(1, 0, 2).reshape(128, 64))}
    pc = lambda g: f(np.asarray(g).reshape(8, 128).T)
    shared["gpc"] = f(np.concatenate([pc(mix_norm[0]), pc(ffn_norm[0]), pc(ple_norm[0])], axis=1))
    go = np.asarray(mix_out_norm[0])
    shared["gconv"] = f(go[0:512].reshape(4, 128).T)
    shared["gattn"] = f(go[512:1024].reshape(8, 64).T)
    shared["cw"] = f(np.asarray(conv_w[0]).reshape(3, 4, 128).transpose(2, 1, 0).reshape(128, 12))
    shared["bfbc"] = f(np.broadcast_to(np.asarray(b_f[0])[None, :], (128, 8)))
    shared["gfin"] = f(np.broadcast_to(np.asarray(final_norm)[None, :], (128, D)))
    bp = np.zeros((128, D), np.float32)
    bp[0, :] = np.asarray(b_ple_gate[0])
    shared["bple"] = bp
    shared.update(_host_consts())

    key = _nblk
    if key not in _NC_CACHE:
        nc = bass.Bass("TRN2", target_bir_lowering=False)
        build(nc, nblk=_nblk)
        _NC_CACHE[key] = nc
    nc = _NC_CACHE[key]
    in_maps = []
    for c in range(NCORES):
        m = dict(shared)
        m["x"] = f(x[2 * c:2 * c + 2])
        m["p"] = f(p[0, 2 * c:2 * c + 2])
        in_maps.append(m)
    res = run_bass_kernel_spmd(nc, in_maps, core_ids=list(range(NCORES)), trace=_trace)
    out = np.concatenate([np.asarray(r["y"], dtype=np.float32) for r in res.results], axis=0)
    if _trace:
        kernel._last = res
    return out
```

```python
import numpy as np
from contextlib import ExitStack
import concourse.bass as bass
import concourse.mybir as mybir
from concourse.bass_utils import run_bass_kernel_spmd

F32 = mybir.dt.float32
BF16 = mybir.dt.bfloat16
AF = mybir.ActivationFunctionType
ALU = mybir.AluOpType

D = 1024
S = 4096
T = 512
DFF = 2816
EPS = 1e-6
NCORES = 8
SEQ_PER_CORE = 2
NEG = -30000.0


class Buf:
    __slots__ = ("w", "r")

    def __init__(self):
        self.w = None
        self.r = {}


class Prog:
    ENG = ("pe", "act", "dve", "pool", "sp")

    def __init__(self):
        self.q = {e: [] for e in self.ENG}
        self.cnt = {e: 0 for e in self.ENG}
        self.dma_cnt = {}

    @staticmethod
    def _deps(reads, writes):
        deps = []
        for b in reads:
            if b.w is not None:
                deps.append(b.w)
        for b in writes:
            if b.w is not None:
                deps.append(b.w)
            deps.extend(b.r.items())
        return deps

    @staticmethod
    def _update(tok, reads, writes):
        for b in writes:
            b.w = tok
            b.r = {}
        for b in reads:
            if not any(b is w for w in writes):
                if b.r.get(tok[0], 0) < tok[1]:
                    b.r[tok[0]] = tok[1]

    def op(self, eng, fn, reads=(), writes=()):
        deps = self._deps(reads, writes)
        self.cnt[eng] += 1
        tok = (eng, self.cnt[eng])
        self.q[eng].append((fn, deps, None))
        self._update(tok, reads, writes)
        return tok

    def dma(self, eng, fn, sem, reads=(), writes=()):
        deps = self._deps(reads, writes)
        self.dma_cnt[sem] = self.dma_cnt.get(sem, 0) + 16
        tok = (sem, self.dma_cnt[sem])
        self.q[eng].append((fn, deps, sem))
        self._update(tok, reads, writes)
        return tok

    def dma_multi(self, eng, fns, sem, reads=(), writes=()):
        deps = self._deps(reads, writes)
        tok = None
        for fn in fns:
            self.dma_cnt[sem] = self.dma_cnt.get(sem, 0) + 16
            tok = (sem, self.dma_cnt[sem])
            self.q[eng].append((fn, deps, sem))
        self._update(tok, reads, writes)
        return tok

    def emit(self, nc, es):
        sems = {}
        for e in self.ENG:
            sems[e] = es.enter_context(nc.semaphore("s_" + e))
        for name in self.dma_cnt:
            sems[name] = es.enter_context(nc.semaphore("d_" + name))
        block = es.enter_context(nc.Block())

        needed = {e: set() for e in self.ENG}
        for engname in self.ENG:
            waited = {}
            for fn, deps, dsem in self.q[engname]:
                for (k, v) in deps:
                    if waited.get(k, 0) < v:
                        waited[k] = v
                        if k in needed:
                            needed[k].add(v)
        rank = {}
        for e in self.ENG:
            rank[e] = {v: i + 1 for i, v in enumerate(sorted(needed[e]))}
        self.n_inc = {e: len(needed[e]) for e in self.ENG}

        def run(engname, eng, final=False):
            waited = {}
            idx = 0
            for fn, deps, dsem in self.q[engname]:
                for (k, v) in deps:
                    if waited.get(k, 0) < v:
                        eng.wait_ge(sems[k], rank[k][v] if k in rank else v)
                        waited[k] = v
                ins = fn(eng)
                if dsem is None:
                    idx += 1
                    if idx in needed[engname]:
                        ins.then_inc(sems[engname], 1)
                else:
                    ins.then_inc(sems[dsem], 16)
            if final:
                for name, v in self.dma_cnt.items():
                    if waited.get(name, 0) < v:
                        eng.wait_ge(sems[name], v)

        @block.tensor
        def _(e):
            run("pe", e)

        @block.scalar
        def _(e):
            run("act", e)

        @block.vector
        def _(e):
            run("dve", e)

        @block.gpsimd
        def _(e):
            run("pool", e)

        @block.sync
        def _(e):
            run("sp", e, final=True)


def block_groups():
    specs = []

    def wspec(kind, **kw):
        d = dict(kind=kind)
        d.update(kw)
        specs.append(d)
        return len(specs) - 1

    d = {}
    d["win"] = {nm: wspec("cols", w="w_in", c0=512 * j) for nm, j in
                (("q", 3), ("k", 4), ("v", 5), ("b", 0), ("c", 1), ("u", 2))}
    d["woc"] = wspec("rows", w="w_o", r0=0, n=4, c0=0, ncol=1024)
    d["woa"] = [wspec("woa", c0=512 * dh) for dh in range(2)]
    d["ffn"] = []
    for (f_lo, f_n) in ((0, 12), (12, 10)):
        gu = [wspec("gu", f0=(f_lo + 2 * j) * 128) for j in range(f_n // 2)]
        dn = []
        n1 = f_n // 2
        for dh in range(2):
            a = wspec("rows", w="w_dn", r0=f_lo * 128, n=n1, c0=512 * dh, ncol=512)
            b = wspec("rows", w="w_dn", r0=(f_lo + n1) * 128, n=f_n - n1, c0=512 * dh, ncol=512)
            dn.append((a, b, n1, f_n - n1))
        d["ffn"].append((f_lo, f_n, gu, dn))
    d["pg"] = [wspec("cols", w="w_pg", c0=512 * dh) for dh in range(2)]
    d["pp"] = wspec("rows", w="w_pp", r0=0, n=2, c0=0, ncol=1024)
    return specs, d


def group_len(g):
    return g["n"] * g["ncol"] if g["kind"] == "rows" else 4096


def pack_weights(ws):
    specs, _ = block_groups()
    out = np.zeros((len(specs), 128, 4096), np.float32)
    for i, g in enumerate(specs):
        k = g["kind"]
        if k == "cols":
            w = ws[g["w"]][:, g["c0"]:g["c0"] + 512]
            out[i] = w.reshape(8, 128, 512).transpose(1, 0, 2).reshape(128, 4096)
        elif k == "gu":
            w = ws["w_gu"]
            f0 = g["f0"]
            both = np.concatenate([w[:, f0:f0 + 256], w[:, DFF + f0:DFF + f0 + 256]], axis=1)
            out[i] = both.reshape(8, 128, 512).transpose(1, 0, 2).reshape(128, 4096)
        elif k == "rows":
            w = ws[g["w"]][g["r0"]:g["r0"] + g["n"] * 128, g["c0"]:g["c0"] + g["ncol"]]
            out[i, :, 0:g["n"] * g["ncol"]] = w.reshape(g["n"], 128, g["ncol"]).transpose(1, 0, 2).reshape(128, -1)
        elif k == "woa":
            w = ws["w_o"][512:1024, g["c0"]:g["c0"] + 512]
            out[i, 0:64, :] = w.reshape(8, 64, 512).transpose(1, 0, 2).reshape(64, 4096)
    return out


def build(nc, nblk=8, nseq=SEQ_PER_CORE):
    P = Prog()
    es = ExitStack()

    def dram(name, shape, kind="ExternalInput"):
        return nc.dram_tensor(name, list(shape), F32, kind=kind).ap()

    x_d = dram("x", [SEQ_PER_CORE, S, D])
    p_d = dram("p", [SEQ_PER_CORE, S, 256])
    bspecs, bplan = block_groups()
    NG = len(bspecs)
    wpack_d = dram("wpack", [NG, 128, 4096])
    wf_d = dram("wf", [128, 64])
    gpc_d = dram("gpc", [128, 24])
    gconv_d = dram("gconv", [128, 4])
    gattn_d = dram("gattn", [64, 8])
    cw_d = dram("cw", [128, 12])
    bfbc_d = dram("bfbc", [128, 8])
    gfin_d = dram("gfin", [128, D])
    bple_d = dram("bple", [128, D])
    ident_d = dram("ident", [128, 128])
    tri_d = dram("tri", [128, 128])
    ones_d = dram("ones", [128, 128])
    maskb_d = dram("maskb", [128, 128])
    bdiag_d = dram("bdiag", [128, 128])
    wn_d = dram("wn", [128, 128])
    onesrow_d = dram("onesrow", [128, 128])
    sel_d = dram("sel", [128, 1024])
    y_d = dram("y", [SEQ_PER_CORE, S, D], kind="ExternalOutput")

    def sb(name, shape, dt):
        return es.enter_context(nc.sbuf_tensor(name, list(shape), dt))

    def psum(name):
        return es.enter_context(nc.psum_tensor(name, [128, 512], F32))

    identb = sb("identb", [128, 128], BF16)
    identf = sb("identf", [128, 128], F32)
    trif = sb("trif", [128, 128], F32)
    onesf = sb("onesf", [128, 128], F32)
    maskb = sb("maskb_s", [128, 128], BF16)
    bdiag = sb("bdiag_s", [128, 128], BF16)
    wn = sb("wn_s", [128, 128], BF16)
    sel = sb("sel_s", [128, 1024], BF16)
    bple = sb("bple_s", [128, D], BF16)
    onesrow = sb("onesrow_s", [128, 128], BF16)
    wf = sb("wf_s", [128, 8, 8], BF16)
    gpc = sb("gpc_s", [128, 24], F32)
    gconv = sb("gconv_s", [128, 4], F32)
    gattn = sb("gattn_s", [64, 8], F32)
    cw = sb("cw_s", [128, 12], F32)
    bfbc = sb("bfbc_s", [128, 8], F32)
    gfin = sb("gfin_s", [128, D], F32)
    Bconst = Buf()

    def cdma(eng, out, in_):
        P.dma(eng, lambda e, out=out, in_=in_: e.dma_start(out=out, in_=in_), "c")

    cdma("sp", identf[:], ident_d[:, :])
    cdma("sp", trif[:], tri_d[:, :])
    cdma("sp", onesf[:], ones_d[:, :])
    cdma("sp", gpc[:], gpc_d[:, :])
    cdma("sp", gconv[:], gconv_d[:, :])
    cdma("sp", gattn[:], gattn_d[:, :])
    cdma("sp", cw[:], cw_d[:, :])
    cdma("sp", bfbc[:], bfbc_d[:, :])
    cdma("sp", gfin[:], gfin_d[:, :])
    cdma("pool", identb[:], ident_d[:, :])
    cdma("pool", maskb[:], maskb_d[:, :])
    cdma("pool", bdiag[:], bdiag_d[:, :])
    cdma("pool", wn[:], wn_d[:, :])
    cdma("pool", sel[:], sel_d[:, :])
    cdma("pool", bple[:], bple_d[:, :])
    cdma("pool", onesrow[:], onesrow_d[:, :])
    cdma("pool", wf[:], wf_d[:, :].rearrange("p (a b) -> p a b", a=8))
    Bconst.w = ("c", P.dma_cnt["c"])

    KT = sb("KT", [128, 4, S], BF16)
    VA = sb("VA", [128, 32, 8, 65], BF16)
    cK = sb("cK", [128, 32, 8], F32)
    carry = sb("carry", [128, 8], F32)
    ccar = sb("ccar", [128, 4, 2], F32)
    BKT = [Buf() for _ in range(4)]
    BVA = Buf()
    BcK = Buf()
    Bcarry = Buf()
    Bccar = Buf()

    hT = [[sb(f"h{b}_{t}", [128, D], F32) for t in range(4)] for b in range(2)]
    Bh = [[Buf() for _ in range(4)] for _ in range(2)]
    xnT = sb("xnT", [128, 8, T], BF16)
    BxnT = Buf()
    xnbf = [sb(f"xnbf{i}", [128, D], BF16) for i in range(2)]
    Bxnbf = [Buf(), Buf()]
    junk = sb("junk", [128, D], BF16)
    Bjunk = Buf()
    stat = sb("stat", [128, 16], F32)
    Bstat = [Buf() for _ in range(4)]
    fst = sb("fst", [128, 32], F32)
    Bfst = Buf()
    QT = sb("QT", [128, 8, T], BF16)
    BQT = [Buf() for _ in range(8)]
    crow = sb("crow", [128, T], BF16)
    Bcrow = Buf()
    NP = 4
    Pb = [sb(f"P{i}", [128, T], BF16) for i in range(NP)]
    BP = [Buf() for _ in range(NP)]
    yT = sb("yT", [128, 4, T], BF16)
    ByT = [Buf() for _ in range(4)]
    yTa = sb("yTa", [128, 8, T], BF16)
    ByTa = [Buf() for _ in range(8)]
    cu = sb("cu", [128, T + 2], F32)
    Bcu = Buf()
    acc = sb("acc", [128, T], F32)
    Bacc = Buf()
    rs = sb("rs", [128, T], F32)
    Brs = Buf()
    rsa, Brsa = rs, Brs
    csb, Bcsb = rs, Brs
    sqb = sb("sqb", [128, T], BF16)
    Bsqb = Buf()
    osq, Bosq = sqb, Bsqb
    NACT = 12
    actT = sb("actT", [128, NACT, T], BF16)
    BactT = [Buf() for _ in range(NACT)]
    sg = [sqb, sb("sg1", [128, T], BF16)]
    Bsg = [Bsqb, Buf()]
    pin = [sb(f"pin{i}", [128, 256], F32) for i in range(2)]
    Bpin = [Buf(), Buf()]
    pbf = sb("pbf", [128, 4, 256], BF16)
    Bpbf = [Buf() for _ in range(4)]
    pT = sb("pT", [128, 2, T], BF16)
    BpT = Buf()
    NSLOT = 4
    wslot = [sb(f"wslot{i}", [128, 4096], BF16) for i in range(NSLOT)]
    Bslot = [Buf() for _ in range(NSLOT)]

    PS = [psum(f"ps{i}") for i in range(8)]
    Bps = [Buf() for _ in range(8)]
    ring = {"mm": 0, "tt": 0, "s": 0, "o": 0, "g": 0}

    def bank(kind="mm"):
        if kind == "mm":
            b = ring["mm"] % 6
        elif kind == "tt":
            b = 6 + ring["tt"] % 2
        elif kind == "s":
            b = ring["s"] % 4
        elif kind == "o":
            b = 4 + ring["o"] % 2
        else:
            b = 6 + ring["g"] % 2
        ring[kind] += 1
        return b

    P.op("dve", lambda e: e.memset(VA[:, :, :, 64:65], 1.0), writes=[BVA])
    P.op("dve", lambda e: e.memset(QT[:], 0.0), writes=BQT)
    P.op("dve", lambda e: e.memset(crow[:], 0.0), writes=[Bcrow])
    P.op("dve", lambda e: e.memset(osq[:], 0.0), writes=[Bosq])
    P.op("dve", lambda e: e.memset(rs[:], 1.0), writes=[Brs])
    P.op("dve", lambda e: e.memset(yTa[:], 0.0), writes=ByTa)

    wstate = {"issued": 0}
    NGROUPS = NG * nseq * nblk
    Bchain = Buf()
    Bchain.w = Bconst.w

    def issue_group(i):
        g = bspecs[i % NG]
        s_ = i % NSLOT
        n = group_len(g)
        o = wslot[s_][:, 0:n]
        i_ = wpack_d[i % NG, :, 0:n]
        wr = [Bslot[s_], Bchain] if i < NSLOT else [Bslot[s_]]
        P.dma("pool", lambda e, o=o, i_=i_: e.dma_start(out=o, in_=i_), f"w{s_}", writes=wr)

    released = set()

    def wpump():
        while wstate["issued"] < NGROUPS:
            j = wstate["issued"]
            if j >= NSLOT and (j - NSLOT) not in released:
                break
            issue_group(j)
            wstate["issued"] += 1

    def wget(i):
        wpump()
        assert wstate["issued"] > i, (i, wstate["issued"])
        return wslot[i % NSLOT], Bslot[i % NSLOT]

    def wrel(*idx):
        for i in idx:
            released.add(i)
        wpump()

    def shift(v, off):
        if isinstance(v, dict):
            return {k: shift(x, off) for k, x in v.items()}
        if isinstance(v, list):
            return [shift(x, off) for x in v]
        return v

    plan = []
    for gb in range(nseq * nblk):
        off = gb * NG
        d = {}
        d["win"] = {k: v + off for k, v in bplan["win"].items()}
        d["woc"] = bplan["woc"] + off
        d["woa"] = [v + off for v in bplan["woa"]]
        d["ffn"] = [(f_lo, f_n, [v + off for v in gu], [(a + off, b + off, na, nb) for (a, b, na, nb) in dn])
                    for (f_lo, f_n, gu, dn) in bplan["ffn"]]
        d["pg"] = [v + off for v in bplan["pg"]]
        d["pp"] = bplan["pp"] + off
        plan.append(d)

    def load_x(sq, blk, hb):
        for tt in range(4):
            r0 = blk * T + tt * 128
            P.dma("sp", lambda e, o=hT[hb][tt][:], i_=x_d[sq, r0:r0 + 128, :]: e.dma_start(out=o, in_=i_),
                  f"h{hb}_{tt}", writes=[Bh[hb][tt]])

    def rstd_all(hb):
        for tt in range(4):
            h = hT[hb][tt]
            P.op("act", lambda e, h=h, tt=tt: e.activation(out=junk[:], in_=h[:], func=AF.Square,
                                                            accum_out=stat[:, tt:tt + 1]),
                 reads=[Bh[hb][tt]], writes=[Bjunk, Bstat[tt]])
        P.op("act", lambda e: e.activation(out=stat[:, 4:8], in_=stat[:, 0:4], func=AF.Ln,
                                           scale=1.0 / D, bias=EPS), reads=[], writes=Bstat)
        P.op("act", lambda e: e.activation(out=stat[:, 8:12], in_=stat[:, 4:8], func=AF.Exp, scale=-0.5),
             reads=[], writes=Bstat)

    def rstd_tiles(hb):
        for tt in range(4):
            h = hT[hb][tt]
            P.op("act", lambda e, h=h, tt=tt: e.activation(out=junk[:], in_=h[:], func=AF.Square,
                                                            accum_out=stat[:, tt:tt + 1]),
                 reads=[Bh[hb][tt]], writes=[Bjunk, Bstat[tt]])
            P.op("act", lambda e, tt=tt: e.activation(out=stat[:, 4 + tt:5 + tt], in_=stat[:, tt:tt + 1], func=AF.Ln,
                                                       scale=1.0 / D, bias=EPS), reads=[], writes=[Bstat[tt]])
            P.op("act", lambda e, tt=tt: e.activation(out=stat[:, 8 + tt:9 + tt], in_=stat[:, 4 + tt:5 + tt],
                                                       func=AF.Exp, scale=-0.5), reads=[], writes=[Bstat[tt]])

    def norm_to_xnT(hb, goff):
        rstd_tiles(hb)
        norm_tail(hb, goff)

    def norm_tail(hb, goff, pre_only=False, skip_pre=False):
        def xn(tt):
            h = hT[hb][tt]
            xb = xnbf[tt % 2]
            P.op("dve", lambda e, h=h, tt=tt, xb=xb: e.tensor_scalar(out=xb[:], in0=h[:],
                                                                      scalar1=stat[:, 8 + tt:9 + tt],
                                                                      scalar2=None, op0=ALU.mult),
                 reads=[Bh[hb][tt], Bstat[tt]], writes=[Bxnbf[tt % 2]])
        if not skip_pre:
            xn(0)
            xn(1)
        if pre_only:
            return
        for tt in range(4):
            xb = xnbf[tt % 2]
            tb = bank("tt")
            pbv = PS[tb][:].bitcast(BF16)

            def tr(e, xb=xb, pbv=pbv):
                ins = None
                for kc in range(8):
                    ins = e.transpose(out=pbv[:, kc * 128:(kc + 1) * 128], in_=xb[:, kc * 128:(kc + 1) * 128],
                                      identity=identb[:])
                return ins
            P.op("pe", tr, reads=[Bxnbf[tt % 2], Bconst], writes=[Bps[tb]])
            if tt + 2 < 4:
                xn(tt + 2)
            gb = gpc[:, goff:goff + 8].unsqueeze(2).to_broadcast([128, 8, 128])
            P.op("dve", lambda e, pbv=pbv, gb=gb, tt=tt: e.tensor_tensor(
                out=xnT[:, :, tt * 128:(tt + 1) * 128], in0=pbv.rearrange("p (k t) -> p k t", k=8), in1=gb,
                op=ALU.mult), reads=[Bps[tb], Bconst], writes=[BxnT])

    def mm_fm(b, wv, Bw, c0):
        def f(e):
            ins = None
            for kc in range(8):
                ins = e.matmul(PS[b][:], lhsT=wv[:, kc, c0:c0 + 128], rhs=xnT[:, kc, :],
                               start=(kc == 0), stop=(kc == 7))
            return ins
        P.op("pe", f, reads=[Bw, BxnT], writes=[Bps[b]])

    def w8(slot):
        return slot[:, 0:4096].rearrange("p (a b) -> p a b", a=8)

    gblk = 0
    load_x(0, 0, 0)
    for sq in range(nseq):
        for blk in range(nblk):
            hb = gblk % 2
            pl = plan[gblk]
            q0 = blk * T
            nxt = gblk + 1
            if nxt < nseq * nblk:
                load_x(nxt // nblk, nxt % nblk, nxt % 2)
            if blk == 0:
                P.op("dve", lambda e: e.memset(carry[:], 0.0), writes=[Bcarry])
                P.op("dve", lambda e: e.memset(ccar[:], 0.0), writes=[Bccar])

            if gblk == 0:
                norm_to_xnT(hb, 0)

            sQ, BQ = wget(pl["win"]["q"])
            for hp in range(4):
                b = bank()
                mm_fm(b, w8(sQ), BQ, hp * 128)
                P.op("act", lambda e, b=b, hp=hp: e.activation(out=QT[0:64, 2 * hp, :], in_=PS[b][0:64, :],
                                                                func=AF.Copy, scale=0.125),
                     reads=[Bps[b]], writes=[BQT[2 * hp]])
                P.op("act", lambda e, b=b, hp=hp: e.activation(out=QT[64:128, 2 * hp + 1, :], in_=PS[b][64:128, :],
                                                                func=AF.Copy, scale=0.125),
                     reads=[Bps[b]], writes=[BQT[2 * hp + 1]])
            wrel(pl["win"]["q"])
            sK, BK = wget(pl["win"]["k"])
            for hp in range(4):
                b = bank()
                mm_fm(b, w8(sK), BK, hp * 128)
                P.op("dve", lambda e, b=b, hp=hp, q0=q0: e.tensor_copy(out=KT[:, hp, q0:q0 + T], in_=PS[b][:]),
                     reads=[Bps[b]], writes=[BKT[hp]])
            wrel(pl["win"]["k"])
            sV, BV = wget(pl["win"]["v"])
            wv = w8(sV)
            def v_mm(tt):
                bv, bz = bank(), bank()

                def fv(e, tt=tt, bv=bv, bz=bz, wv=wv):
                    ins = None
                    for kc in range(8):
                        e.matmul(PS[bv][:], lhsT=xnT[:, kc, tt * 128:(tt + 1) * 128], rhs=wv[:, kc, :],
                                 start=(kc == 0), stop=(kc == 7))
                    for kc in range(8):
                        ins = e.matmul(PS[bz][:, 0:8], lhsT=xnT[:, kc, tt * 128:(tt + 1) * 128], rhs=wf[:, kc, :],
                                       start=(kc == 0), stop=(kc == 7))
                    return ins
                P.op("pe", fv, reads=[BxnT, BV, Bconst], writes=[Bps[bv], Bps[bz]])
                return bv, bz

            def v_post1(tt, bv, bz):
                kt = blk * 4 + tt
                P.op("dve", lambda e, kt=kt, bv=bv: e.tensor_copy(
                    out=VA[:, kt, :, 0:64], in_=PS[bv][:].rearrange("p (h d) -> p h d", h=8)),
                    reads=[Bps[bv]], writes=[BVA])
                P.op("dve", lambda e, bz=bz: e.tensor_tensor(out=fst[:, 0:8], in0=PS[bz][:, 0:8], in1=bfbc[:],
                                                              op=ALU.add),
                     reads=[Bps[bz], Bconst], writes=[Bfst])
                P.op("act", lambda e: e.activation(out=fst[:, 8:16], in_=fst[:, 0:8], func=AF.Exp, scale=-1.0),
                     reads=[], writes=[Bfst])
                P.op("act", lambda e: e.activation(out=fst[:, 16:24], in_=fst[:, 8:16], func=AF.Ln, bias=1.0),
                     reads=[], writes=[Bfst])

            def v_post2(tt):
                kt = blk * 4 + tt
                bc = bank()

                def fc_(e, bc=bc):
                    e.matmul(PS[bc][:, 0:8], lhsT=trif[:], rhs=fst[:, 16:24], start=True, stop=True)
                    return e.matmul(PS[bc][:, 8:16], lhsT=onesf[:], rhs=fst[:, 16:24], start=True, stop=True)
                P.op("pe", fc_, reads=[Bfst, Bconst], writes=[Bps[bc]])
                P.op("dve", lambda e, bc=bc, kt=kt: e.tensor_tensor(out=cK[:, kt, :], in0=PS[bc][:, 0:8],
                                                                    in1=carry[:], op=ALU.add),
                     reads=[Bps[bc], Bcarry], writes=[BcK])
                P.op("dve", lambda e, bc=bc: e.tensor_tensor(out=carry[:], in0=PS[bc][:, 8:16], in1=carry[:],
                                                              op=ALU.add),
                     reads=[Bps[bc]], writes=[Bcarry])

            vb = v_mm(0)
            for tt in range(4):
                v_post1(tt, *vb)
                if tt + 1 < 4:
                    vb = v_mm(tt + 1)
                v_post2(tt)
            wrel(pl["win"]["v"])
            bx = bank()

            def ftr(e, bx=bx, blk=blk):
                ins = None
                for tt in range(4):
                    ins = e.transpose(out=PS[bx][0:8, tt * 128:(tt + 1) * 128], in_=cK[:, blk * 4 + tt, :],
                                      identity=identf[:])
                return ins
            P.op("pe", ftr, reads=[BcK, Bconst], writes=[Bps[bx]])
            P.op("act", lambda e, bx=bx: e.activation(out=crow[0:8, :], in_=PS[bx][0:8, :], func=AF.Copy, scale=-1.0),
                 reads=[Bps[bx]], writes=[Bcrow])

            gb_, gc_, gu_ = pl["win"]["b"], pl["win"]["c"], pl["win"]["u"]
            sB, BB = wget(gb_)
            sC, BC = wget(gc_)
            sU, BU = wget(gu_)
            def conv_mm(c):
                b_b, b_c, b_u = bank(), bank(), bank()
                mm_fm(b_b, w8(sB), BB, c * 128)
                mm_fm(b_c, w8(sC), BC, c * 128)
                mm_fm(b_u, w8(sU), BU, c * 128)
                return b_b, b_c, b_u

            def conv_square():
                P.op("act", lambda e: e.activation(out=sqb[:], in_=acc[:], func=AF.Square),
                     reads=[Bacc], writes=[Bsqb])

            def conv_chain(c, banks, square=True):
                b_b, b_c, b_u = banks
                P.op("act", lambda e, b_c=b_c: e.activation(out=csb[:], in_=PS[b_c][:], func=AF.Copy),
                     reads=[Bps[b_c]], writes=[Bcsb])
                P.op("dve", lambda e, c=c: e.tensor_copy(out=cu[:, 0:2], in_=ccar[:, c, :]),
                     reads=[Bccar], writes=[Bcu])
                P.op("dve", lambda e, b_u=b_u: e.tensor_tensor(out=cu[:, 2:T + 2], in0=csb[:], in1=PS[b_u][:],
                                                                op=ALU.mult),
                     reads=[Bcsb, Bps[b_u]], writes=[Bcu])
                P.op("dve", lambda e, c=c: e.tensor_copy(out=ccar[:, c, :], in_=cu[:, T:T + 2]),
                     reads=[Bcu], writes=[Bccar])
                P.op("dve", lambda e, c=c: e.tensor_scalar(out=acc[:], in0=cu[:, 2:T + 2],
                                                           scalar1=cw[:, c * 3 + 2:c * 3 + 3], scalar2=None,
                                                           op0=ALU.mult),
                     reads=[Bcu, Bconst], writes=[Bacc])
                P.op("dve", lambda e, c=c: e.scalar_tensor_tensor(out=acc[:], in0=cu[:, 1:T + 1],
                                                                  scalar=cw[:, c * 3 + 1:c * 3 + 2], in1=acc[:],
                                                                  op0=ALU.mult, op1=ALU.add),
                     reads=[Bcu, Bconst], writes=[Bacc])
                P.op("dve", lambda e, c=c: e.scalar_tensor_tensor(out=acc[:], in0=cu[:, 0:T],
                                                                  scalar=cw[:, c * 3:c * 3 + 1], in1=acc[:],
                                                                  op0=ALU.mult, op1=ALU.add),
                     reads=[Bcu, Bconst], writes=[Bacc])
                P.op("dve", lambda e, b_b=b_b: e.tensor_tensor(out=acc[:], in0=acc[:], in1=PS[b_b][:], op=ALU.mult),
                     reads=[Bps[b_b]], writes=[Bacc])
                if square:
                    conv_square()

            def conv_fin(c):
                gbk = bank("g")
                P.op("pe", lambda e, gbk=gbk: e.matmul(PS[gbk][:], lhsT=bdiag[:], rhs=sqb[:], start=True, stop=True),
                     reads=[Bsqb, Bconst], writes=[Bps[gbk]])
                P.op("act", lambda e, gbk=gbk: e.activation(out=rs[:], in_=PS[gbk][:], func=AF.Ln, bias=EPS),
                     reads=[Bps[gbk]], writes=[Brs])
                P.op("act", lambda e: e.activation(out=rs[:], in_=rs[:], func=AF.Exp, scale=-0.5),
                     reads=[], writes=[Brs])
                P.op("dve", lambda e, c=c: e.scalar_tensor_tensor(out=yT[:, c, :], in0=acc[:],
                                                                  scalar=gconv[:, c:c + 1], in1=rs[:],
                                                                  op0=ALU.mult, op1=ALU.mult),
                     reads=[Bacc, Brs, Bconst], writes=[ByT[c]])

            cbanks = conv_mm(0)
            for c in range(3):
                conv_chain(c, cbanks)
                cbanks = conv_mm(c + 1)
                conv_fin(c)
            wrel(gb_, gc_, gu_)
            conv_chain(3, cbanks, square=False)
            conv_late = {2: conv_square, 4: (lambda: conv_fin(3))}
            units = []
            for h in range(8):
                lst = [(kt, 0, False) for kt in range(blk * 4)] + [(blk * 4 + j, 128 * j, True) for j in range(4)]
                for ui, (kt, c0, dg) in enumerate(lst):
                    units.append((h, kt, c0, dg, ui == 0, ui == len(lst) - 1))
            LOOK = 2
            obank = {}
            slots = {}
            deferred = []

            def post_pe(h, ob):
                gbk = bank("g")
                P.op("pe", lambda e, gbk=gbk: e.matmul(PS[gbk][:, :], lhsT=wn[:, :], rhs=osq[:, :],
                                                         start=True, stop=True),
                     reads=[Bosq, Bconst], writes=[Bps[gbk]])
                P.op("act", lambda e, gbk=gbk: e.activation(out=rsa[0:64, :], in_=PS[gbk][0:64, :], func=AF.Ln),
                     reads=[Bps[gbk]], writes=[Brsa])
                P.op("act", lambda e: e.activation(out=rsa[0:64, :], in_=rsa[0:64, :], func=AF.Exp, scale=-0.5),
                     reads=[], writes=[Brsa])
                P.op("dve", lambda e, h=h, ob=ob: e.scalar_tensor_tensor(
                    out=yTa[0:64, h, :], in0=PS[ob][0:64, :], scalar=gattn[:, h:h + 1], in1=rsa[0:64, :],
                    op0=ALU.mult, op1=ALU.mult), reads=[Bps[ob], Brsa, Bconst], writes=[ByTa[h]])

            for i in range(len(units) + LOOK):
                if i < len(units):
                    h, kt, c0, dg, first, last = units[i]
                    hp, s_ = h // 2, h % 2
                    rows = slice(64 * s_, 64 * s_ + 64)
                    sbk = bank("s")
                    pi = i % NP
                    slots[i] = pi

                    def fqk(e, sbk=sbk, hp=hp, rows=rows, kt=kt, c0=c0, dg=dg, h=h):
                        e.matmul(PS[sbk][:, c0:T], lhsT=KT[:, hp, kt * 128:(kt + 1) * 128],
                                 rhs=QT[:, h, c0:T], start=True, stop=False)
                        ins = e.matmul(PS[sbk][:, c0:T], lhsT=sel[:, h * 128:(h + 1) * 128], rhs=crow[:, c0:T],
                                       start=False, stop=(not dg))
                        if dg:
                            ins = e.matmul(PS[sbk][:, c0:c0 + 128], lhsT=identb[:], rhs=maskb[:],
                                           start=False, stop=True)
                        return ins
                    P.op("pe", fqk, reads=[BKT[hp], BQT[h], Bcrow, Bconst], writes=[Bps[sbk]])
                    P.op("act", lambda e, sbk=sbk, pi=pi, c0=c0, kt=kt, h=h: e.activation(
                        out=Pb[pi][:, c0:T], in_=PS[sbk][:, c0:T], func=AF.Exp, bias=cK[:, kt, h:h + 1], scale=1.0),
                        reads=[Bps[sbk], BcK], writes=[BP[pi]])
                j = i - LOOK
                if j >= 0:
                    h, kt, c0, dg, first, last = units[j]
                    if first:
                        obank[h] = bank("o")
                    ob = obank[h]
                    pi = slots[j]
                    P.op("pe", lambda e, ob=ob, kt=kt, h=h, pi=pi, c0=c0, first=first, last=last: e.matmul(
                        PS[ob][0:65, c0:T], lhsT=VA[:, kt, h, :], rhs=Pb[pi][:, c0:T], start=first, stop=last),
                        reads=[BP[pi], BVA], writes=[Bps[ob]])
                    if last:
                        P.op("act", lambda e, ob=ob: e.activation(out=osq[0:65, :], in_=PS[ob][0:65, :],
                                                                   func=AF.Square),
                             reads=[Bps[ob]], writes=[Bosq])
                        deferred.append((i + 2, (lambda h=h, ob=ob: post_pe(h, ob))))
                if i in conv_late:
                    conv_late.pop(i)()
                while deferred and deferred[0][0] <= i:
                    deferred.pop(0)[1]()
            while deferred:
                deferred.pop(0)[1]()

            for tt in range(4):
                r0 = q0 + tt * 128
                P.dma("sp", lambda e, o=pin[tt % 2][:], i_=p_d[sq, r0:r0 + 128, :]: e.dma_start(out=o, in_=i_),
                      f"pin{tt % 2}", writes=[Bpin[tt % 2]])
                P.op("dve", lambda e, tt=tt: e.tensor_copy(out=pbf[:, tt, :], in_=pin[tt % 2][:]),
                     reads=[Bpin[tt % 2]], writes=[Bpbf[tt]])

            sWc, BWc = wget(pl["woc"])
            wcv = sWc[:, 0:4096].rearrange("p (a b) -> p a b", a=4)
            woa_ = [wget(pl["woa"][dh]) for dh in range(2)]
            for tt in range(4):
                for dh in range(2):
                    sWa, BWa = woa_[dh]
                    wav = sWa[:, 0:4096].rearrange("p (a b) -> p a b", a=8)
                    b = bank()

                    def fo(e, b=b, tt=tt, dh=dh, wav=wav, wcv=wcv):
                        for c in range(4):
                            e.matmul(PS[b][:], lhsT=yT[:, c, tt * 128:(tt + 1) * 128],
                                     rhs=wcv[:, c, dh * 512:(dh + 1) * 512], start=(c == 0), stop=False)
                        ins = None
                        for h in range(8):
                            ins = e.matmul(PS[b][:], lhsT=yTa[:, h, tt * 128:(tt + 1) * 128], rhs=wav[:, h, :],
                                           start=False, stop=(h == 7))
                        return ins
                    P.op("pe", fo, reads=ByT + ByTa + [BWc, BWa], writes=[Bps[b]])
                    hh = hT[hb][tt]
                    P.op("dve", lambda e, b=b, hh=hh, dh=dh: e.tensor_tensor(
                        out=hh[:, dh * 512:(dh + 1) * 512], in0=hh[:, dh * 512:(dh + 1) * 512], in1=PS[b][:],
                        op=ALU.add), reads=[Bps[b]], writes=[Bh[hb][tt]])
            wrel(pl["woa"][0], pl["woa"][1], pl["woc"])

            norm_to_xnT(hb, 8)
            for (f_lo, f_n, gu, dn) in pl["ffn"]:
                for j in range(f_n):
                    sG, BG = wget(gu[j // 2])
                    gv = w8(sG)
                    bg, bu = bank(), bank()
                    mm_fm(bg, gv, BG, (j % 2) * 128)
                    mm_fm(bu, gv, BG, 256 + (j % 2) * 128)
                    si = j % 2
                    P.op("act", lambda e, bg=bg, si=si: e.activation(out=sg[si][:], in_=PS[bg][:], func=AF.Silu),
                         reads=[Bps[bg]], writes=[Bsg[si]])
                    P.op("dve", lambda e, bu=bu, si=si, j=j: e.tensor_tensor(out=actT[:, j, :], in0=sg[si][:],
                                                                              in1=PS[bu][:], op=ALU.mult),
                         reads=[Bsg[si], Bps[bu]], writes=[BactT[j]])
                    if j % 2 == 1:
                        wrel(gu[j // 2])
                for dh in range(2):
                    ga, gb2, na, nb = dn[dh]
                    sA, BA = wget(ga)
                    sB2, BB2 = wget(gb2)
                    av = sA[:, 0:na * 512].rearrange("p (a b) -> p a b", a=na)
                    bv_ = sB2[:, 0:nb * 512].rearrange("p (a b) -> p a b", a=nb)
                    for tt in range(4):
                        b = bank()

                        def fd(e, b=b, tt=tt, av=av, bv_=bv_, na=na, nb=nb):
                            ins = None
                            for j in range(na + nb):
                                rhs = av[:, j, :] if j < na else bv_[:, j - na, :]
                                ins = e.matmul(PS[b][:], lhsT=actT[:, j, tt * 128:(tt + 1) * 128], rhs=rhs,
                                               start=(j == 0), stop=(j == na + nb - 1))
                            return ins
                        P.op("pe", fd, reads=BactT[0:f_n] + [BA, BB2], writes=[Bps[b]])
                        hh = hT[hb][tt]
                        P.op("dve", lambda e, b=b, hh=hh, dh=dh: e.tensor_tensor(
                            out=hh[:, dh * 512:(dh + 1) * 512], in0=hh[:, dh * 512:(dh + 1) * 512], in1=PS[b][:],
                            op=ALU.add), reads=[Bps[b]], writes=[Bh[hb][tt]])
                    wrel(ga, gb2)

            norm_to_xnT(hb, 16)
            tb = bank("tt")
            pbv = PS[tb][:].bitcast(BF16)

            def ftp(e, pbv=pbv):
                ins = None
                for pc in range(2):
                    for tt in range(4):
                        o0 = (pc * 4 + tt) * 128
                        ins = e.transpose(out=pbv[:, o0:o0 + 128], in_=pbf[:, tt, pc * 128:(pc + 1) * 128],
                                          identity=identb[:])
                return ins
            P.op("pe", ftp, reads=Bpbf + [Bconst], writes=[Bps[tb]])
            P.op("dve", lambda e, pbv=pbv: e.tensor_copy(out=pT[:, :, :], in_=pbv.rearrange("p (c t) -> p c t", c=2)),
                 reads=[Bps[tb]], writes=[BpT])
            if nxt < nseq * nblk:
                rstd_all(nxt % 2)
                norm_tail(nxt % 2, 0, pre_only=True)
            sPP, BPP = None, None
            for dh in range(2):
                sPG, BPG = wget(pl["pg"][dh])
                if dh == 0:
                    sPP, BPP = wget(pl["pp"])
                pgv = w8(sPG)
                ppv = sPP[:, 0:2048].rearrange("p (a b) -> p a b", a=2)
                for tt in range(4):
                    bgt, bpp = bank(), bank()

                    def fg(e, bgt=bgt, bpp=bpp, tt=tt, dh=dh, pgv=pgv, ppv=ppv):
                        for kc in range(8):
                            e.matmul(PS[bgt][:], lhsT=xnT[:, kc, tt * 128:(tt + 1) * 128], rhs=pgv[:, kc, :],
                                     start=(kc == 0), stop=False)
                        e.matmul(PS[bgt][:], lhsT=onesrow[:, :], rhs=bple[:, dh * 512:(dh + 1) * 512],
                                 start=False, stop=True)
                        ins = None
                        for pc in range(2):
                            ins = e.matmul(PS[bpp][:], lhsT=pT[:, pc, tt * 128:(tt + 1) * 128],
                                           rhs=ppv[:, pc, dh * 512:(dh + 1) * 512], start=(pc == 0), stop=(pc == 1))
                        return ins
                    P.op("pe", fg, reads=[BxnT, BpT, BPG, BPP, Bconst], writes=[Bps[bgt], Bps[bpp]])
                    P.op("act", lambda e, bgt=bgt: e.activation(out=acc[:], in_=PS[bgt][:], func=AF.Sigmoid),
                         reads=[Bps[bgt]], writes=[Bacc])
                    P.op("dve", lambda e, bpp=bpp: e.tensor_tensor(out=acc[:], in0=acc[:], in1=PS[bpp][:],
                                                                    op=ALU.mult),
                         reads=[Bps[bpp]], writes=[Bacc])
                    hh = hT[hb][tt]
                    P.op("dve", lambda e, hh=hh, dh=dh: e.tensor_tensor(
                        out=hh[:, dh * 512:(dh + 1) * 512], in0=hh[:, dh * 512:(dh + 1) * 512], in1=acc[:],
                        op=ALU.add), reads=[Bacc], writes=[Bh[hb][tt]])
                wrel(pl["pg"][dh])
            wrel(pl["pp"])

            if nxt < nseq * nblk:
                norm_tail(nxt % 2, 0, skip_pre=True)

            rstd_all(hb)
            for tt in range(4):
                h = hT[hb][tt]
                P.op("dve", lambda e, h=h, tt=tt: e.scalar_tensor_tensor(out=h[:], in0=h[:],
                                                                          scalar=stat[:, 8 + tt:9 + tt],
                                                                          in1=gfin[:], op0=ALU.mult, op1=ALU.mult),
                     reads=[Bstat[tt], Bconst], writes=[Bh[hb][tt]])
                r0 = q0 + tt * 128
                P.dma("sp", lambda e, h=h, o=y_d[sq, r0:r0 + 128, :]: e.dma_start(out=o, in_=h[:]),
                      f"h{hb}_{tt}", reads=[Bh[hb][tt]])
            gblk += 1

    P.emit(nc, es)
    P.sbuf_left = nc.sbuf_bytes_remaining
    es.close()
    return P


def _host_consts():
    c = {}
    c["ident"] = np.eye(128, dtype=np.float32)
    j = np.arange(128)
    c["tri"] = (j[:, None] <= j[None, :]).astype(np.float32)
    c["ones"] = np.ones((128, 128), np.float32)
    c["maskb"] = np.where(j[:, None] <= j[None, :], 0.0, NEG).astype(np.float32)
    c["bdiag"] = ((j[:, None] // 64) == (j[None, :] // 64)).astype(np.float32) / 64.0
    wn = np.zeros((128, 128), np.float32)
    wn[0:64, 0:64] = 1.0 / 64.0
    wn[64, 0:64] = EPS
    c["wn"] = wn
    sel = np.zeros((128, 8, 128), np.float32)
    for h in range(8):
        sel[h, h, :] = 1.0
    c["sel"] = sel.reshape(128, 1024)
    orow = np.zeros((128, 128), np.float32)
    orow[0, :] = 1.0
    c["onesrow"] = orow
    return c


_NC_CACHE = {}


def kernel(x, p, mix_norm, w_in, b_f, conv_w, mix_out_norm, w_o, ffn_norm, w_gate_up, w_down,
           ple_norm, w_ple_gate, b_ple_gate, w_ple_proj, final_norm, _nblk=8, _trace=False):
    f = lambda a: np.ascontiguousarray(np.asarray(a, dtype=np.float32))
    x = f(x); p = f(p)
    ws = {"w_in": f(w_in[0]), "w_o": f(w_o[0]), "w_gu": f(w_gate_up[0]), "w_dn": f(w_down[0]),
          "w_pg": f(w_ple_gate[0]), "w_pp": f(w_ple_proj[0])}
    shared = {"wpack": pack_weights(ws),
              "wf": f(ws["w_in"][:, 3072:3080].reshape(8, 128, 8).transpose(1, 0, 2).reshape(128, 64))}
    pc = lambda g: f(np.asarray(g).reshape(8, 128).T)
    shared["gpc"] = f(np.concatenate([pc(mix_norm[0]), pc(ffn_norm[0]), pc(ple_norm[0])], axis=1))
    go = np.asarray(mix_out_norm[0])
    shared["gconv"] = f(go[0:512].reshape(4, 128).T)
    shared["gattn"] = f(go[512:1024].reshape(8, 64).T)
    shared["cw"] = f(np.asarray(conv_w[0]).reshape(3, 4, 128).transpose(2, 1, 0).reshape(128, 12))
    shared["bfbc"] = f(np.broadcast_to(np.asarray(b_f[0])[None, :], (128, 8)))
    shared["gfin"] = f(np.broadcast_to(np.asarray(final_norm)[None, :], (128, D)))
    bp = np.zeros((128, D), np.float32)
    bp[0, :] = np.asarray(b_ple_gate[0])
    shared["bple"] = bp
    shared.update(_host_consts())

    key = _nblk
    if key not in _NC_CACHE:
        nc = bass.Bass("TRN2", target_bir_lowering=False)
        build(nc, nblk=_nblk)
        _NC_CACHE[key] = nc
    nc = _NC_CACHE[key]
    in_maps = []
    for c in range(NCORES):
        m = dict(shared)
        m["x"] = f(x[2 * c:2 * c + 2])
        m["p"] = f(p[0, 2 * c:2 * c + 2])
        in_maps.append(m)
    res = run_bass_kernel_spmd(nc, in_maps, core_ids=list(range(NCORES)), trace=_trace)
    out = np.concatenate([np.asarray(r["y"], dtype=np.float32) for r in res.results], axis=0)
    if _trace:
        kernel._last = res
    return out
```

```python
import numpy as np
from contextlib import ExitStack
import concourse.bass as bass
import concourse.mybir as mybir
from concourse.bass_utils import run_bass_kernel_spmd

F32 = mybir.dt.float32
BF16 = mybir.dt.bfloat16
AF = mybir.ActivationFunctionType
ALU = mybir.AluOpType

D = 1024
S = 4096
T = 512
DFF = 2816
EPS = 1e-6
NCORES = 8
SEQ_PER_CORE = 2
NEG = -30000.0


class Buf:
    __slots__ = ("w", "r")

    def __init__(self):
        self.w = None
        self.r = {}


class Prog:
    ENG = ("pe", "act", "dve", "pool", "sp")

    def __init__(self):
        self.q = {e: [] for e in self.ENG}
        self.cnt = {e: 0 for e in self.ENG}
        self.dma_cnt = {}

    @staticmethod
    def _deps(reads, writes):
        deps = []
        for b in reads:
            if b.w is not None:
                deps.append(b.w)
        for b in writes:
            if b.w is not None:
                deps.append(b.w)
            deps.extend(b.r.items())
        return deps

    @staticmethod
    def _update(tok, reads, writes):
        for b in writes:
            b.w = tok
            b.r = {}
        for b in reads:
            if not any(b is w for w in writes):
                if b.r.get(tok[0], 0) < tok[1]:
                    b.r[tok[0]] = tok[1]

    def op(self, eng, fn, reads=(), writes=()):
        deps = self._deps(reads, writes)
        self.cnt[eng] += 1
        tok = (eng, self.cnt[eng])
        self.q[eng].append((fn, deps, None))
        self._update(tok, reads, writes)
        return tok

    def dma(self, eng, fn, sem, reads=(), writes=()):
        deps = self._deps(reads, writes)
        self.dma_cnt[sem] = self.dma_cnt.get(sem, 0) + 16
        tok = (sem, self.dma_cnt[sem])
        self.q[eng].append((fn, deps, sem))
        self._update(tok, reads, writes)
        return tok

    def dma_multi(self, eng, fns, sem, reads=(), writes=()):
        deps = self._deps(reads, writes)
        tok = None
        for fn in fns:
            self.dma_cnt[sem] = self.dma_cnt.get(sem, 0) + 16
            tok = (sem, self.dma_cnt[sem])
            self.q[eng].append((fn, deps, sem))
        self._update(tok, reads, writes)
        return tok

    def emit(self, nc, es):
        sems = {}
        for e in self.ENG:
            sems[e] = es.enter_context(nc.semaphore("s_" + e))
        for name in self.dma_cnt:
            sems[name] = es.enter_context(nc.semaphore("d_" + name))
        block = es.enter_context(nc.Block())

        needed = {e: set() for e in self.ENG}
        for engname in self.ENG:
            waited = {}
            for fn, deps, dsem in self.q[engname]:
                for (k, v) in deps:
                    if waited.get(k, 0) < v:
                        waited[k] = v
                        if k in needed:
                            needed[k].add(v)
        rank = {}
        for e in self.ENG:
            rank[e] = {v: i + 1 for i, v in enumerate(sorted(needed[e]))}
        self.n_inc = {e: len(needed[e]) for e in self.ENG}

        def run(engname, eng, final=False):
            waited = {}
            idx = 0
            for fn, deps, dsem in self.q[engname]:
                for (k, v) in deps:
                    if waited.get(k, 0) < v:
                        eng.wait_ge(sems[k], rank[k][v] if k in rank else v)
                        waited[k] = v
                ins = fn(eng)
                if dsem is None:
                    idx += 1
                    if idx in needed[engname]:
                        ins.then_inc(sems[engname], 1)
                else:
                    ins.then_inc(sems[dsem], 16)
            if final:
                for name, v in self.dma_cnt.items():
                    if waited.get(name, 0) < v:
                        eng.wait_ge(sems[name], v)

        @block.tensor
        def _(e):
            run("pe", e)

        @block.scalar
        def _(e):
            run("act", e)

        @block.vector
        def _(e):
            run("dve", e)

        @block.gpsimd
        def _(e):
            run("pool", e)

        @block.sync
        def _(e):
            run("sp", e, final=True)


def block_groups():
    specs = []

    def wspec(kind, **kw):
        d = dict(kind=kind)
        d.update(kw)
        specs.append(d)
        return len(specs) - 1

    d = {}
    d["win"] = {nm: wspec("cols", w="w_in", c0=512 * j) for nm, j in
                (("q", 3), ("k", 4), ("v", 5), ("b", 0), ("c", 1), ("u", 2))}
    d["woc"] = wspec("rows", w="w_o", r0=0, n=4, c0=0, ncol=1024)
    d["woa"] = [wspec("woa", c0=512 * dh) for dh in range(2)]
    d["ffn"] = []
    for (f_lo, f_n) in ((0, 12), (12, 10)):
        gu = [wspec("gu", f0=(f_lo + 2 * j) * 128) for j in range(f_n // 2)]
        dn = []
        n1 = f_n // 2
        for dh in range(2):
            a = wspec("rows", w="w_dn", r0=f_lo * 128, n=n1, c0=512 * dh, ncol=512)
            b = wspec("rows", w="w_dn", r0=(f_lo + n1) * 128, n=f_n - n1, c0=512 * dh, ncol=512)
            dn.append((a, b, n1, f_n - n1))
        d["ffn"].append((f_lo, f_n, gu, dn))
    d["pg"] = [wspec("cols", w="w_pg", c0=512 * dh) for dh in range(2)]
    d["pp"] = wspec("rows", w="w_pp", r0=0, n=2, c0=0, ncol=1024)
    return specs, d


def group_len(g):
    return g["n"] * g["ncol"] if g["kind"] == "rows" else 4096


def pack_weights(ws):
    specs, _ = block_groups()
    out = np.zeros((len(specs), 128, 4096), np.float32)
    for i, g in enumerate(specs):
        k = g["kind"]
        if k == "cols":
            w = ws[g["w"]][:, g["c0"]:g["c0"] + 512]
            out[i] = w.reshape(8, 128, 512).transpose(1, 0, 2).reshape(128, 4096)
        elif k == "gu":
            w = ws["w_gu"]
            f0 = g["f0"]
            both = np.concatenate([w[:, f0:f0 + 256], w[:, DFF + f0:DFF + f0 + 256]], axis=1)
            out[i] = both.reshape(8, 128, 512).transpose(1, 0, 2).reshape(128, 4096)
        elif k == "rows":
            w = ws[g["w"]][g["r0"]:g["r0"] + g["n"] * 128, g["c0"]:g["c0"] + g["ncol"]]
            out[i, :, 0:g["n"] * g["ncol"]] = w.reshape(g["n"], 128, g["ncol"]).transpose(1, 0, 2).reshape(128, -1)
        elif k == "woa":
            w = ws["w_o"][512:1024, g["c0"]:g["c0"] + 512]
            out[i, 0:64, :] = w.reshape(8, 64, 512).transpose(1, 0, 2).reshape(64, 4096)
    return out


def build(nc, nblk=8, nseq=SEQ_PER_CORE):
    P = Prog()
    es = ExitStack()

    def dram(name, shape, kind="ExternalInput"):
        return nc.dram_tensor(name, list(shape), F32, kind=kind).ap()

    x_d = dram("x", [SEQ_PER_CORE, S, D])
    p_d = dram("p", [SEQ_PER_CORE, S, 256])
    bspecs, bplan = block_groups()
    NG = len(bspecs)
    wpack_d = dram("wpack", [NG, 128, 4096])
    wf_d = dram("wf", [128, 64])
    gpc_d = dram("gpc", [128, 24])
    gconv_d = dram("gconv", [128, 4])
    gattn_d = dram("gattn", [64, 8])
    cw_d = dram("cw", [128, 12])
    bfbc_d = dram("bfbc", [128, 8])
    gfin_d = dram("gfin", [128, D])
    bple_d = dram("bple", [128, D])
    ident_d = dram("ident", [128, 128])
    tri_d = dram("tri", [128, 128])
    ones_d = dram("ones", [128, 128])
    maskb_d = dram("maskb", [128, 128])
    bdiag_d = dram("bdiag", [128, 128])
    wn_d = dram("wn", [128, 128])
    onesrow_d = dram("onesrow", [128, 128])
    sel_d = dram("sel", [128, 1024])
    y_d = dram("y", [SEQ_PER_CORE, S, D], kind="ExternalOutput")

    def sb(name, shape, dt):
        return es.enter_context(nc.sbuf_tensor(name, list(shape), dt))

    def psum(name):
        return es.enter_context(nc.psum_tensor(name, [128, 512], F32))

    identb = sb("identb", [128, 128], BF16)
    identf = sb("identf", [128, 128], F32)
    trif = sb("trif", [128, 128], F32)
    onesf = sb("onesf", [128, 128], F32)
    maskb = sb("maskb_s", [128, 128], BF16)
    bdiag = sb("bdiag_s", [128, 128], BF16)
    wn = sb("wn_s", [128, 128], BF16)
    sel = sb("sel_s", [128, 1024], BF16)
    bple = sb("bple_s", [128, D], BF16)
    onesrow = sb("onesrow_s", [128, 128], BF16)
    wf = sb("wf_s", [128, 8, 8], BF16)
    gpc = sb("gpc_s", [128, 24], F32)
    gconv = sb("gconv_s", [128, 4], F32)
    gattn = sb("gattn_s", [64, 8], F32)
    cw = sb("cw_s", [128, 12], F32)
    bfbc = sb("bfbc_s", [128, 8], F32)
    gfin = sb("gfin_s", [128, D], F32)
    Bconst = Buf()

    def cdma(eng, out, in_):
        P.dma(eng, lambda e, out=out, in_=in_: e.dma_start(out=out, in_=in_), "c_" + eng)

    cdma("sp", identf[:], ident_d[:, :])
    cdma("sp", trif[:], tri_d[:, :])
    cdma("sp", onesf[:], ones_d[:, :])
    cdma("sp", gpc[:], gpc_d[:, :])
    cdma("sp", gconv[:], gconv_d[:, :])
    cdma("sp", gattn[:], gattn_d[:, :])
    cdma("sp", cw[:], cw_d[:, :])
    cdma("sp", bfbc[:], bfbc_d[:, :])
    cdma("sp", gfin[:], gfin_d[:, :])
    cdma("pool", identb[:], ident_d[:, :])
    cdma("pool", maskb[:], maskb_d[:, :])
    cdma("pool", bdiag[:], bdiag_d[:, :])
    cdma("pool", wn[:], wn_d[:, :])
    cdma("pool", sel[:], sel_d[:, :])
    cdma("pool", bple[:], bple_d[:, :])
    cdma("pool", onesrow[:], onesrow_d[:, :])
    cdma("pool", wf[:], wf_d[:, :].rearrange("p (a b) -> p a b", a=8))
    Bc_sp, Bc_pool = Buf(), Buf()
    Bc_sp.w = ("c_sp", P.dma_cnt["c_sp"])
    Bc_pool.w = ("c_pool", P.dma_cnt["c_pool"])
    cdummy = sb("cdummy", [128, 4], F32)
    P.op("dve", lambda e: e.memset(cdummy[:], 0.0), reads=[Bc_sp, Bc_pool], writes=[Bconst])

    KT = sb("KT", [128, 4, S], BF16)
    VA = sb("VA", [128, 32, 8, 65], BF16)
    cK = sb("cK", [128, 32, 8], F32)
    carry = sb("carry", [128, 8], F32)
    ccar = sb("ccar", [128, 4, 2], F32)
    BKT = [Buf() for _ in range(4)]
    BVA = Buf()
    BcK = Buf()
    Bcarry = Buf()
    Bccar = Buf()

    hT = [[sb(f"h{b}_{t}", [128, D], F32) for t in range(4)] for b in range(2)]
    Bh = [[Buf() for _ in range(4)] for _ in range(2)]
    xnT = sb("xnT", [128, 8, T], BF16)
    BxnT = Buf()
    xnbf = [sb(f"xnbf{i}", [128, D], BF16) for i in range(2)]
    Bxnbf = [Buf(), Buf()]
    junk = sb("junk", [128, D], BF16)
    Bjunk = Buf()
    stat = sb("stat", [128, 16], F32)
    Bstat = [Buf() for _ in range(4)]
    fst = sb("fst", [128, 32], F32)
    Bfst = Buf()
    QT = sb("QT", [128, 8, T], BF16)
    BQT = [Buf() for _ in range(8)]
    crow = sb("crow", [128, T], BF16)
    Bcrow = Buf()
    NP = 4
    Pb = [sb(f"P{i}", [128, T], BF16) for i in range(NP)]
    BP = [Buf() for _ in range(NP)]
    yT = sb("yT", [128, 4, T], BF16)
    ByT = [Buf() for _ in range(4)]
    yTa = sb("yTa", [128, 8, T], BF16)
    ByTa = [Buf() for _ in range(8)]
    cu = sb("cu", [128, T + 2], F32)
    Bcu = Buf()
    acc = sb("acc", [128, T], F32)
    Bacc = Buf()
    rs = sb("rs", [128, T], F32)
    Brs = Buf()
    rsa, Brsa = rs, Brs
    csb, Bcsb = rs, Brs
    sqb = sb("sqb", [128, T], BF16)
    Bsqb = Buf()
    osq, Bosq = sqb, Bsqb
    NACT = 12
    actT = sb("actT", [128, NACT, T], BF16)
    BactT = [Buf() for _ in range(NACT)]
    sg = [sqb, sb("sg1", [128, T], BF16)]
    Bsg = [Bsqb, Buf()]
    pin = [sb(f"pin{i}", [128, 256], F32) for i in range(2)]
    Bpin = [Buf(), Buf()]
    pbf = sb("pbf", [128, 4, 256], BF16)
    Bpbf = [Buf() for _ in range(4)]
    pT = sb("pT", [128, 2, T], BF16)
    BpT = Buf()
    NSLOT = 4
    wslot = [sb(f"wslot{i}", [128, 4096], BF16) for i in range(NSLOT)]
    Bslot = [Buf() for _ in range(NSLOT)]

    PS = [psum(f"ps{i}") for i in range(8)]
    Bps = [Buf() for _ in range(8)]
    ring = {"mm": 0, "tt": 0, "s": 0, "o": 0, "g": 0, "sorder": [0, 1, 2, 3]}

    def bank(kind="mm"):
        if kind == "mm":
            b = ring["mm"] % 6
        elif kind == "tt":
            b = 6 + ring["tt"] % 2
        elif kind == "s":
            b = ring["sorder"][ring["s"] % 4]
        elif kind == "o":
            b = 4 + ring["o"] % 2
        else:
            b = 6 + ring["g"] % 2
        ring[kind] += 1
        return b

    P.op("dve", lambda e: e.memset(VA[:, :, :, 64:65], 1.0), writes=[BVA])
    P.op("dve", lambda e: e.memset(QT[:], 0.0), writes=BQT)
    P.op("dve", lambda e: e.memset(crow[:], 0.0), writes=[Bcrow])
    P.op("dve", lambda e: e.memset(osq[:], 0.0), writes=[Bosq])
    P.op("dve", lambda e: e.memset(rs[:], 1.0), writes=[Brs])
    P.op("dve", lambda e: e.memset(yTa[:], 0.0), writes=ByTa)

    wstate = {"issued": 0}
    NGROUPS = NG * nseq * nblk
    Bchain = Buf()
    Bchain.w = Bc_pool.w

    def issue_group(i):
        g = bspecs[i % NG]
        s_ = i % NSLOT
        n = group_len(g)
        o = wslot[s_][:, 0:n]
        i_ = wpack_d[i % NG, :, 0:n]
        wr = [Bslot[s_], Bchain] if i < NSLOT else [Bslot[s_]]
        P.dma("pool", lambda e, o=o, i_=i_: e.dma_start(out=o, in_=i_), f"w{s_}", writes=wr)

    released = set()

    def wpump():
        while wstate["issued"] < NGROUPS:
            j = wstate["issued"]
            if j >= NSLOT and (j - NSLOT) not in released:
                break
            issue_group(j)
            wstate["issued"] += 1

    def wget(i):
        wpump()
        assert wstate["issued"] > i, (i, wstate["issued"])
        return wslot[i % NSLOT], Bslot[i % NSLOT]

    def wrel(*idx):
        for i in idx:
            released.add(i)
        wpump()

    def shift(v, off):
        if isinstance(v, dict):
            return {k: shift(x, off) for k, x in v.items()}
        if isinstance(v, list):
            return [shift(x, off) for x in v]
        return v

    plan = []
    for gb in range(nseq * nblk):
        off = gb * NG
        d = {}
        d["win"] = {k: v + off for k, v in bplan["win"].items()}
        d["woc"] = bplan["woc"] + off
        d["woa"] = [v + off for v in bplan["woa"]]
        d["ffn"] = [(f_lo, f_n, [v + off for v in gu], [(a + off, b + off, na, nb) for (a, b, na, nb) in dn])
                    for (f_lo, f_n, gu, dn) in bplan["ffn"]]
        d["pg"] = [v + off for v in bplan["pg"]]
        d["pp"] = bplan["pp"] + off
        plan.append(d)

    def load_x(sq, blk, hb):
        for tt in range(4):
            r0 = blk * T + tt * 128
            P.dma("sp", lambda e, o=hT[hb][tt][:], i_=x_d[sq, r0:r0 + 128, :]: e.dma_start(out=o, in_=i_),
                  f"h{hb}_{tt}", writes=[Bh[hb][tt]])

    def rstd_all(hb):
        for tt in range(4):
            h = hT[hb][tt]
            P.op("act", lambda e, h=h, tt=tt: e.activation(out=junk[:], in_=h[:], func=AF.Square,
                                                            accum_out=stat[:, tt:tt + 1]),
                 reads=[Bh[hb][tt]], writes=[Bjunk, Bstat[tt]])
        P.op("act", lambda e: e.activation(out=stat[:, 4:8], in_=stat[:, 0:4], func=AF.Ln,
                                           scale=1.0 / D, bias=EPS), reads=[], writes=Bstat)
        P.op("act", lambda e: e.activation(out=stat[:, 8:12], in_=stat[:, 4:8], func=AF.Exp, scale=-0.5),
             reads=[], writes=Bstat)

    def rstd_tiles(hb):
        for tt in range(4):
            h = hT[hb][tt]
            P.op("act", lambda e, h=h, tt=tt: e.activation(out=junk[:], in_=h[:], func=AF.Square,
                                                            accum_out=stat[:, tt:tt + 1]),
                 reads=[Bh[hb][tt]], writes=[Bjunk, Bstat[tt]])
            P.op("act", lambda e, tt=tt: e.activation(out=stat[:, 4 + tt:5 + tt], in_=stat[:, tt:tt + 1], func=AF.Ln,
                                                       scale=1.0 / D, bias=EPS), reads=[], writes=[Bstat[tt]])
            P.op("act", lambda e, tt=tt: e.activation(out=stat[:, 8 + tt:9 + tt], in_=stat[:, 4 + tt:5 + tt],
                                                       func=AF.Exp, scale=-0.5), reads=[], writes=[Bstat[tt]])

    def norm_to_xnT(hb, goff):
        rstd_tiles(hb)
        norm_tail(hb, goff)

    def norm_tail(hb, goff, pre_only=False, skip_pre=False):
        def xn(tt):
            h = hT[hb][tt]
            xb = xnbf[tt % 2]
            P.op("dve", lambda e, h=h, tt=tt, xb=xb: e.tensor_scalar(out=xb[:], in0=h[:],
                                                                      scalar1=stat[:, 8 + tt:9 + tt],
                                                                      scalar2=None, op0=ALU.mult),
                 reads=[Bh[hb][tt], Bstat[tt]], writes=[Bxnbf[tt % 2]])
        if not skip_pre:
            xn(0)
            xn(1)
        if pre_only:
            return
        for tt in range(4):
            xb = xnbf[tt % 2]
            tb = bank("tt")
            pbv = PS[tb][:].bitcast(BF16)

            def tr(e, xb=xb, pbv=pbv):
                ins = None
                for kc in range(8):
                    ins = e.transpose(out=pbv[:, kc * 128:(kc + 1) * 128], in_=xb[:, kc * 128:(kc + 1) * 128],
                                      identity=identb[:])
                return ins
            P.op("pe", tr, reads=[Bxnbf[tt % 2], Bconst], writes=[Bps[tb]])
            if tt + 2 < 4:
                xn(tt + 2)
            gb = gpc[:, goff:goff + 8].unsqueeze(2).to_broadcast([128, 8, 128])
            P.op("dve", lambda e, pbv=pbv, gb=gb, tt=tt: e.tensor_tensor(
                out=xnT[:, :, tt * 128:(tt + 1) * 128], in0=pbv.rearrange("p (k t) -> p k t", k=8), in1=gb,
                op=ALU.mult), reads=[Bps[tb], Bconst], writes=[BxnT])

    def mm_fm(b, wv, Bw, c0):
        def f(e):
            ins = None
            for kc in range(8):
                ins = e.matmul(PS[b][:], lhsT=wv[:, kc, c0:c0 + 128], rhs=xnT[:, kc, :],
                               start=(kc == 0), stop=(kc == 7))
            return ins
        P.op("pe", f, reads=[Bw, BxnT], writes=[Bps[b]])

    def w8(slot):
        return slot[:, 0:4096].rearrange("p (a b) -> p a b", a=8)

    gblk = 0
    load_x(0, 0, 0)
    for sq in range(nseq):
        for blk in range(nblk):
            hb = gblk % 2
            pl = plan[gblk]
            q0 = blk * T
            nxt = gblk + 1
            if nxt < nseq * nblk:
                load_x(nxt // nblk, nxt % nblk, nxt % 2)
            if blk == 0:
                P.op("dve", lambda e: e.memset(carry[:], 0.0), writes=[Bcarry])
                P.op("dve", lambda e: e.memset(ccar[:], 0.0), writes=[Bccar])

            if gblk == 0:
                norm_to_xnT(hb, 0)

            sQ, BQ = wget(pl["win"]["q"])
            for hp in range(4):
                b = bank()
                mm_fm(b, w8(sQ), BQ, hp * 128)
                P.op("act", lambda e, b=b, hp=hp: e.activation(out=QT[0:64, 2 * hp, :], in_=PS[b][0:64, :],
                                                                func=AF.Copy, scale=0.125),
                     reads=[Bps[b]], writes=[BQT[2 * hp]])
                P.op("act", lambda e, b=b, hp=hp: e.activation(out=QT[64:128, 2 * hp + 1, :], in_=PS[b][64:128, :],
                                                                func=AF.Copy, scale=0.125),
                     reads=[Bps[b]], writes=[BQT[2 * hp + 1]])
            wrel(pl["win"]["q"])
            sK, BK = wget(pl["win"]["k"])
            for hp in range(4):
                b = bank()
                mm_fm(b, w8(sK), BK, hp * 128)
                P.op("dve", lambda e, b=b, hp=hp, q0=q0: e.tensor_copy(out=KT[:, hp, q0:q0 + T], in_=PS[b][:]),
                     reads=[Bps[b]], writes=[BKT[hp]])
            wrel(pl["win"]["k"])
            sV, BV = wget(pl["win"]["v"])
            wv = w8(sV)
            def v_mm(tt):
                bv, bz = bank(), bank()

                def fv(e, tt=tt, bv=bv, bz=bz, wv=wv):
                    ins = None
                    for kc in range(8):
                        e.matmul(PS[bv][:], lhsT=xnT[:, kc, tt * 128:(tt + 1) * 128], rhs=wv[:, kc, :],
                                 start=(kc == 0), stop=(kc == 7))
                    for kc in range(8):
                        ins = e.matmul(PS[bz][:, 0:8], lhsT=xnT[:, kc, tt * 128:(tt + 1) * 128], rhs=wf[:, kc, :],
                                       start=(kc == 0), stop=(kc == 7))
                    return ins
                P.op("pe", fv, reads=[BxnT, BV, Bconst], writes=[Bps[bv], Bps[bz]])
                return bv, bz

            def v_post1(tt, bv, bz):
                kt = blk * 4 + tt
                P.op("dve", lambda e, kt=kt, bv=bv: e.tensor_copy(
                    out=VA[:, kt, :, 0:64], in_=PS[bv][:].rearrange("p (h d) -> p h d", h=8)),
                    reads=[Bps[bv]], writes=[BVA])
                P.op("dve", lambda e, bz=bz: e.tensor_tensor(out=fst[:, 0:8], in0=PS[bz][:, 0:8], in1=bfbc[:],
                                                              op=ALU.add),
                     reads=[Bps[bz], Bconst], writes=[Bfst])
                P.op("act", lambda e: e.activation(out=fst[:, 8:16], in_=fst[:, 0:8], func=AF.Exp, scale=-1.0),
                     reads=[], writes=[Bfst])
                P.op("act", lambda e: e.activation(out=fst[:, 16:24], in_=fst[:, 8:16], func=AF.Ln, bias=1.0),
                     reads=[], writes=[Bfst])

            def v_post2(tt):
                kt = blk * 4 + tt
                bc = bank()

                def fc_(e, bc=bc):
                    e.matmul(PS[bc][:, 0:8], lhsT=trif[:], rhs=fst[:, 16:24], start=True, stop=True)
                    return e.matmul(PS[bc][:, 8:16], lhsT=onesf[:], rhs=fst[:, 16:24], start=True, stop=True)
                P.op("pe", fc_, reads=[Bfst, Bconst], writes=[Bps[bc]])
                P.op("dve", lambda e, bc=bc, kt=kt: e.tensor_tensor(out=cK[:, kt, :], in0=PS[bc][:, 0:8],
                                                                    in1=carry[:], op=ALU.add),
                     reads=[Bps[bc], Bcarry], writes=[BcK])
                P.op("dve", lambda e, bc=bc: e.tensor_tensor(out=carry[:], in0=PS[bc][:, 8:16], in1=carry[:],
                                                              op=ALU.add),
                     reads=[Bps[bc]], writes=[Bcarry])

            vb = v_mm(0)
            for tt in range(4):
                v_post1(tt, *vb)
                if tt + 1 < 4:
                    vb = v_mm(tt + 1)
                v_post2(tt)
            wrel(pl["win"]["v"])
            bx = bank()

            def ftr(e, bx=bx, blk=blk):
                ins = None
                for tt in range(4):
                    ins = e.transpose(out=PS[bx][0:8, tt * 128:(tt + 1) * 128], in_=cK[:, blk * 4 + tt, :],
                                      identity=identf[:])
                return ins
            P.op("pe", ftr, reads=[BcK, Bconst], writes=[Bps[bx]])
            P.op("act", lambda e, bx=bx: e.activation(out=crow[0:8, :], in_=PS[bx][0:8, :], func=AF.Copy, scale=-1.0),
                 reads=[Bps[bx]], writes=[Bcrow])

            gb_, gc_, gu_ = pl["win"]["b"], pl["win"]["c"], pl["win"]["u"]
            sB, BB = wget(gb_)
            sC, BC = wget(gc_)
            sU, BU = wget(gu_)
            def conv_mm(c):
                b_b, b_c, b_u = bank(), bank(), bank()
                mm_fm(b_b, w8(sB), BB, c * 128)
                mm_fm(b_c, w8(sC), BC, c * 128)
                mm_fm(b_u, w8(sU), BU, c * 128)
                return b_b, b_c, b_u

            def conv_square():
                P.op("act", lambda e: e.activation(out=sqb[:], in_=acc[:], func=AF.Square),
                     reads=[Bacc], writes=[Bsqb])

            def conv_chain(c, banks, square=True):
                b_b, b_c, b_u = banks
                P.op("act", lambda e, b_c=b_c: e.activation(out=csb[:], in_=PS[b_c][:], func=AF.Copy),
                     reads=[Bps[b_c]], writes=[Bcsb])
                P.op("dve", lambda e, c=c: e.tensor_copy(out=cu[:, 0:2], in_=ccar[:, c, :]),
                     reads=[Bccar], writes=[Bcu])
                P.op("dve", lambda e, b_u=b_u: e.tensor_tensor(out=cu[:, 2:T + 2], in0=csb[:], in1=PS[b_u][:],
                                                                op=ALU.mult),
                     reads=[Bcsb, Bps[b_u]], writes=[Bcu])
                P.op("dve", lambda e, c=c: e.tensor_copy(out=ccar[:, c, :], in_=cu[:, T:T + 2]),
                     reads=[Bcu], writes=[Bccar])
                P.op("dve", lambda e, c=c: e.tensor_scalar(out=acc[:], in0=cu[:, 2:T + 2],
                                                           scalar1=cw[:, c * 3 + 2:c * 3 + 3], scalar2=None,
                                                           op0=ALU.mult),
                     reads=[Bcu, Bconst], writes=[Bacc])
                P.op("dve", lambda e, c=c: e.scalar_tensor_tensor(out=acc[:], in0=cu[:, 1:T + 1],
                                                                  scalar=cw[:, c * 3 + 1:c * 3 + 2], in1=acc[:],
                                                                  op0=ALU.mult, op1=ALU.add),
                     reads=[Bcu, Bconst], writes=[Bacc])
                P.op("dve", lambda e, c=c: e.scalar_tensor_tensor(out=acc[:], in0=cu[:, 0:T],
                                                                  scalar=cw[:, c * 3:c * 3 + 1], in1=acc[:],
                                                                  op0=ALU.mult, op1=ALU.add),
                     reads=[Bcu, Bconst], writes=[Bacc])
                P.op("dve", lambda e, b_b=b_b: e.tensor_tensor(out=acc[:], in0=acc[:], in1=PS[b_b][:], op=ALU.mult),
                     reads=[Bps[b_b]], writes=[Bacc])
                if square:
                    conv_square()

            def conv_fin(c):
                gbk = bank("g")
                P.op("pe", lambda e, gbk=gbk: e.matmul(PS[gbk][:], lhsT=bdiag[:], rhs=sqb[:], start=True, stop=True),
                     reads=[Bsqb, Bconst], writes=[Bps[gbk]])
                P.op("act", lambda e, gbk=gbk: e.activation(out=rs[:], in_=PS[gbk][:], func=AF.Ln, bias=EPS),
                     reads=[Bps[gbk]], writes=[Brs])
                P.op("act", lambda e: e.activation(out=rs[:], in_=rs[:], func=AF.Exp, scale=-0.5),
                     reads=[], writes=[Brs])
                P.op("dve", lambda e, c=c: e.scalar_tensor_tensor(out=yT[:, c, :], in0=acc[:],
                                                                  scalar=gconv[:, c:c + 1], in1=rs[:],
                                                                  op0=ALU.mult, op1=ALU.mult),
                     reads=[Bacc, Brs, Bconst], writes=[ByT[c]])

            cbanks = conv_mm(0)
            for c in range(3):
                conv_chain(c, cbanks)
                cbanks = conv_mm(c + 1)
                conv_fin(c)
            wrel(gb_, gc_, gu_)
            conv_chain(3, cbanks, square=False)
            ring["s"] = 0
            ring["sorder"] = [b for b in range(4) if b not in cbanks] + [b for b in range(4) if b in cbanks]
            conv_late = {2: conv_square, 4: (lambda: conv_fin(3))}
            units = []
            for h in range(8):
                lst = [(kt, 0, False) for kt in range(blk * 4)] + [(blk * 4 + j, 128 * j, True) for j in range(4)]
                for ui, (kt, c0, dg) in enumerate(lst):
                    units.append((h, kt, c0, dg, ui == 0, ui == len(lst) - 1))
            LOOK = 3
            obank = {}
            slots = {}
            deferred = []

            def post_pe(h, ob):
                gbk = bank("g")
                P.op("pe", lambda e, gbk=gbk: e.matmul(PS[gbk][:, :], lhsT=wn[:, :], rhs=osq[:, :],
                                                         start=True, stop=True),
                     reads=[Bosq, Bconst], writes=[Bps[gbk]])
                P.op("act", lambda e, gbk=gbk: e.activation(out=rsa[0:64, :], in_=PS[gbk][0:64, :], func=AF.Ln),
                     reads=[Bps[gbk]], writes=[Brsa])
                P.op("act", lambda e: e.activation(out=rsa[0:64, :], in_=rsa[0:64, :], func=AF.Exp, scale=-0.5),
                     reads=[], writes=[Brsa])
                P.op("dve", lambda e, h=h, ob=ob: e.scalar_tensor_tensor(
                    out=yTa[0:64, h, :], in0=PS[ob][0:64, :], scalar=gattn[:, h:h + 1], in1=rsa[0:64, :],
                    op0=ALU.mult, op1=ALU.mult), reads=[Bps[ob], Brsa, Bconst], writes=[ByTa[h]])

            for i in range(len(units) + LOOK):
                if i < len(units):
                    h, kt, c0, dg, first, last = units[i]
                    hp, s_ = h // 2, h % 2
                    rows = slice(64 * s_, 64 * s_ + 64)
                    sbk = bank("s")
                    pi = i % NP
                    slots[i] = pi

                    def fqk(e, sbk=sbk, hp=hp, rows=rows, kt=kt, c0=c0, dg=dg, h=h):
                        e.matmul(PS[sbk][:, c0:T], lhsT=KT[:, hp, kt * 128:(kt + 1) * 128],
                                 rhs=QT[:, h, c0:T], start=True, stop=False)
                        ins = e.matmul(PS[sbk][:, c0:T], lhsT=sel[:, h * 128:(h + 1) * 128], rhs=crow[:, c0:T],
                                       start=False, stop=(not dg))
                        if dg:
                            ins = e.matmul(PS[sbk][:, c0:c0 + 128], lhsT=identb[:], rhs=maskb[:],
                                           start=False, stop=True)
                        return ins
                    P.op("pe", fqk, reads=[BKT[hp], BQT[h], Bcrow, Bconst], writes=[Bps[sbk]])
                    P.op("act", lambda e, sbk=sbk, pi=pi, c0=c0, kt=kt, h=h: e.activation(
                        out=Pb[pi][:, c0:T], in_=PS[sbk][:, c0:T], func=AF.Exp, bias=cK[:, kt, h:h + 1], scale=1.0),
                        reads=[Bps[sbk], BcK], writes=[BP[pi]])
                j = i - LOOK
                if j >= 0:
                    h, kt, c0, dg, first, last = units[j]
                    if first:
                        obank[h] = bank("o")
                    ob = obank[h]
                    pi = slots[j]
                    P.op("pe", lambda e, ob=ob, kt=kt, h=h, pi=pi, c0=c0, first=first, last=last: e.matmul(
                        PS[ob][0:65, c0:T], lhsT=VA[:, kt, h, :], rhs=Pb[pi][:, c0:T], start=first, stop=last),
                        reads=[BP[pi], BVA], writes=[Bps[ob]])
                    if last:
                        P.op("act", lambda e, ob=ob: e.activation(out=osq[0:65, :], in_=PS[ob][0:65, :],
                                                                   func=AF.Square),
                             reads=[Bps[ob]], writes=[Bosq])
                        deferred.append((i + 2, (lambda h=h, ob=ob: post_pe(h, ob))))
                if i in conv_late:
                    conv_late.pop(i)()
                while deferred and deferred[0][0] <= i:
                    deferred.pop(0)[1]()
            while deferred:
                deferred.pop(0)[1]()

            for tt in range(4):
                r0 = q0 + tt * 128
                P.dma("sp", lambda e, o=pin[tt % 2][:], i_=p_d[sq, r0:r0 + 128, :]: e.dma_start(out=o, in_=i_),
                      f"pin{tt % 2}", writes=[Bpin[tt % 2]])
                P.op("dve", lambda e, tt=tt: e.tensor_copy(out=pbf[:, tt, :], in_=pin[tt % 2][:]),
                     reads=[Bpin[tt % 2]], writes=[Bpbf[tt]])

            sWc, BWc = wget(pl["woc"])
            wcv = sWc[:, 0:4096].rearrange("p (a b) -> p a b", a=4)
            woa_ = [wget(pl["woa"][dh]) for dh in range(2)]
            for tt in range(4):
                for dh in range(2):
                    sWa, BWa = woa_[dh]
                    wav = sWa[:, 0:4096].rearrange("p (a b) -> p a b", a=8)
                    b = bank()

                    def fo(e, b=b, tt=tt, dh=dh, wav=wav, wcv=wcv):
                        for c in range(4):
                            e.matmul(PS[b][:], lhsT=yT[:, c, tt * 128:(tt + 1) * 128],
                                     rhs=wcv[:, c, dh * 512:(dh + 1) * 512], start=(c == 0), stop=False)
                        ins = None
                        for h in range(6):
                            ins = e.matmul(PS[b][:], lhsT=yTa[:, h, tt * 128:(tt + 1) * 128], rhs=wav[:, h, :],
                                           start=False, stop=False)
                        return ins

                    def fo2(e, b=b, tt=tt, wav=wav):
                        ins = None
                        for h in range(6, 8):
                            ins = e.matmul(PS[b][:], lhsT=yTa[:, h, tt * 128:(tt + 1) * 128], rhs=wav[:, h, :],
                                           start=False, stop=(h == 7))
                        return ins
                    P.op("pe", fo, reads=ByT + ByTa[0:6] + [BWc, BWa], writes=[Bps[b]])
                    P.op("pe", fo2, reads=ByTa[6:8] + [BWa], writes=[Bps[b]])
                    hh = hT[hb][tt]
                    P.op("dve", lambda e, b=b, hh=hh, dh=dh: e.tensor_tensor(
                        out=hh[:, dh * 512:(dh + 1) * 512], in0=hh[:, dh * 512:(dh + 1) * 512], in1=PS[b][:],
                        op=ALU.add), reads=[Bps[b]], writes=[Bh[hb][tt]])
            wrel(pl["woa"][0], pl["woa"][1], pl["woc"])

            norm_to_xnT(hb, 8)
            for (f_lo, f_n, gu, dn) in pl["ffn"]:
                for j in range(f_n):
                    sG, BG = wget(gu[j // 2])
                    gv = w8(sG)
                    bg, bu = bank(), bank()
                    mm_fm(bg, gv, BG, (j % 2) * 128)
                    mm_fm(bu, gv, BG, 256 + (j % 2) * 128)
                    si = j % 2
                    P.op("act", lambda e, bg=bg, si=si: e.activation(out=sg[si][:], in_=PS[bg][:], func=AF.Silu),
                         reads=[Bps[bg]], writes=[Bsg[si]])
                    P.op("dve", lambda e, bu=bu, si=si, j=j: e.tensor_tensor(out=actT[:, j, :], in0=sg[si][:],
                                                                              in1=PS[bu][:], op=ALU.mult),
                         reads=[Bsg[si], Bps[bu]], writes=[BactT[j]])
                    if j % 2 == 1:
                        wrel(gu[j // 2])
                for dh in range(2):
                    ga, gb2, na, nb = dn[dh]
                    sA, BA = wget(ga)
                    sB2, BB2 = wget(gb2)
                    av = sA[:, 0:na * 512].rearrange("p (a b) -> p a b", a=na)
                    bv_ = sB2[:, 0:nb * 512].rearrange("p (a b) -> p a b", a=nb)
                    for tt in range(4):
                        b = bank()

                        def fd(e, b=b, tt=tt, av=av, bv_=bv_, na=na, nb=nb):
                            ins = None
                            for j in range(na + nb):
                                rhs = av[:, j, :] if j < na else bv_[:, j - na, :]
                                ins = e.matmul(PS[b][:], lhsT=actT[:, j, tt * 128:(tt + 1) * 128], rhs=rhs,
                                               start=(j == 0), stop=(j == na + nb - 1))
                            return ins
                        P.op("pe", fd, reads=BactT[0:f_n] + [BA, BB2], writes=[Bps[b]])
                        hh = hT[hb][tt]
                        P.op("dve", lambda e, b=b, hh=hh, dh=dh: e.tensor_tensor(
                            out=hh[:, dh * 512:(dh + 1) * 512], in0=hh[:, dh * 512:(dh + 1) * 512], in1=PS[b][:],
                            op=ALU.add), reads=[Bps[b]], writes=[Bh[hb][tt]])
                    wrel(ga, gb2)

            norm_to_xnT(hb, 16)
            tb = bank("tt")
            pbv = PS[tb][:].bitcast(BF16)

            def ftp(e, pbv=pbv):
                ins = None
                for pc in range(2):
                    for tt in range(4):
                        o0 = (pc * 4 + tt) * 128
                        ins = e.transpose(out=pbv[:, o0:o0 + 128], in_=pbf[:, tt, pc * 128:(pc + 1) * 128],
                                          identity=identb[:])
                return ins
            P.op("pe", ftp, reads=Bpbf + [Bconst], writes=[Bps[tb]])
            P.op("dve", lambda e, pbv=pbv: e.tensor_copy(out=pT[:, :, :], in_=pbv.rearrange("p (c t) -> p c t", c=2)),
                 reads=[Bps[tb]], writes=[BpT])
            if nxt < nseq * nblk:
                rstd_all(nxt % 2)
                norm_tail(nxt % 2, 0, pre_only=True)
            sPP, BPP = None, None
            for dh in range(2):
                sPG, BPG = wget(pl["pg"][dh])
                if dh == 0:
                    sPP, BPP = wget(pl["pp"])
                pgv = w8(sPG)
                ppv = sPP[:, 0:2048].rearrange("p (a b) -> p a b", a=2)
                for tt in range(4):
                    bgt, bpp = bank(), bank()

                    def fg(e, bgt=bgt, bpp=bpp, tt=tt, dh=dh, pgv=pgv, ppv=ppv):
                        for kc in range(8):
                            e.matmul(PS[bgt][:], lhsT=xnT[:, kc, tt * 128:(tt + 1) * 128], rhs=pgv[:, kc, :],
                                     start=(kc == 0), stop=False)
                        e.matmul(PS[bgt][:], lhsT=onesrow[:, :], rhs=bple[:, dh * 512:(dh + 1) * 512],
                                 start=False, stop=True)
                        ins = None
                        for pc in range(2):
                            ins = e.matmul(PS[bpp][:], lhsT=pT[:, pc, tt * 128:(tt + 1) * 128],
                                           rhs=ppv[:, pc, dh * 512:(dh + 1) * 512], start=(pc == 0), stop=(pc == 1))
                        return ins
                    P.op("pe", fg, reads=[BxnT, BpT, BPG, BPP, Bconst], writes=[Bps[bgt], Bps[bpp]])
                    P.op("act", lambda e, bgt=bgt: e.activation(out=acc[:], in_=PS[bgt][:], func=AF.Sigmoid),
                         reads=[Bps[bgt]], writes=[Bacc])
                    P.op("dve", lambda e, bpp=bpp: e.tensor_tensor(out=acc[:], in0=acc[:], in1=PS[bpp][:],
                                                                    op=ALU.mult),
                         reads=[Bps[bpp]], writes=[Bacc])
                    hh = hT[hb][tt]
                    P.op("dve", lambda e, hh=hh, dh=dh: e.tensor_tensor(
                        out=hh[:, dh * 512:(dh + 1) * 512], in0=hh[:, dh * 512:(dh + 1) * 512], in1=acc[:],
                        op=ALU.add), reads=[Bacc], writes=[Bh[hb][tt]])
                wrel(pl["pg"][dh])
            wrel(pl["pp"])

            if nxt < nseq * nblk:
                norm_tail(nxt % 2, 0, skip_pre=True)

            rstd_all(hb)
            for tt in range(4):
                h = hT[hb][tt]
                P.op("dve", lambda e, h=h, tt=tt: e.scalar_tensor_tensor(out=h[:], in0=h[:],
                                                                          scalar=stat[:, 8 + tt:9 + tt],
                                                                          in1=gfin[:], op0=ALU.mult, op1=ALU.mult),
                     reads=[Bstat[tt], Bconst], writes=[Bh[hb][tt]])
                r0 = q0 + tt * 128
                P.dma("sp", lambda e, h=h, o=y_d[sq, r0:r0 + 128, :]: e.dma_start(out=o, in_=h[:]),
                      f"h{hb}_{tt}", reads=[Bh[hb][tt]])
            gblk += 1

    P.emit(nc, es)
    P.sbuf_left = nc.sbuf_bytes_remaining
    es.close()
    return P


def _host_consts():
    c = {}
    c["ident"] = np.eye(128, dtype=np.float32)
    j = np.arange(128)
    c["tri"] = (j[:, None] <= j[None, :]).astype(np.float32)
    c["ones"] = np.ones((128, 128), np.float32)
    c["maskb"] = np.where(j[:, None] <= j[None, :], 0.0, NEG).astype(np.float32)
    c["bdiag"] = ((j[:, None] // 64) == (j[None, :] // 64)).astype(np.float32) / 64.0
    wn = np.zeros((128, 128), np.float32)
    wn[0:64, 0:64] = 1.0 / 64.0
    wn[64, 0:64] = EPS
    c["wn"] = wn
    sel = np.zeros((128, 8, 128), np.float32)
    for h in range(8):
        sel[h, h, :] = 1.0
    c["sel"] = sel.reshape(128, 1024)
    orow = np.zeros((128, 128), np.float32)
    orow[0, :] = 1.0
    c["onesrow"] = orow
    return c


_NC_CACHE = {}


def kernel(x, p, mix_norm, w_in, b_f, conv_w, mix_out_norm, w_o, ffn_norm, w_gate_up, w_down,
           ple_norm, w_ple_gate, b_ple_gate, w_ple_proj, final_norm, _nblk=8, _trace=False):
    f = lambda a: np.ascontiguousarray(np.asarray(a, dtype=np.float32))
    x = f(x); p = f(p)
    ws = {"w_in": f(w_in[0]), "w_o": f(w_o[0]), "w_gu": f(w_gate_up[0]), "w_dn": f(w_down[0]),
          "w_pg": f(w_ple_gate[0]), "w_pp": f(w_ple_proj[0])}
    shared = {"wpack": pack_weights(ws),
              "wf": f(ws["w_in"][:, 3072:3080].reshape(8, 128, 8).transpose(1, 0, 2).reshape(128, 64))}
    pc = lambda g: f(np.asarray(g).reshape(8, 128).T)
    shared["gpc"] = f(np.concatenate([pc(mix_norm[0]), pc(ffn_norm[0]), pc(ple_norm[0])], axis=1))
    go = np.asarray(mix_out_norm[0])
    shared["gconv"] = f(go[0:512].reshape(4, 128).T)
    shared["gattn"] = f(go[512:1024].reshape(8, 64).T)
    shared["cw"] = f(np.asarray(conv_w[0]).reshape(3, 4, 128).transpose(2, 1, 0).reshape(128, 12))
    shared["bfbc"] = f(np.broadcast_to(np.asarray(b_f[0])[None, :], (128, 8)))
    shared["gfin"] = f(np.broadcast_to(np.asarray(final_norm)[None, :], (128, D)))
    bp = np.zeros((128, D), np.float32)
    bp[0, :] = np.asarray(b_ple_gate[0])
    shared["bple"] = bp
    shared.update(_host_consts())

    key = _nblk
    if key not in _NC_CACHE:
        nc = bass.Bass("TRN2", target_bir_lowering=False)
        build(nc, nblk=_nblk)
        _NC_CACHE[key] = nc
    nc = _NC_CACHE[key]
    in_maps = []
    for c in range(NCORES):
        m = dict(shared)
        m["x"] = f(x[2 * c:2 * c + 2])
        m["p"] = f(p[0, 2 * c:2 * c + 2])
        in_maps.append(m)
    res = run_bass_kernel_spmd(nc, in_maps, core_ids=list(range(NCORES)), trace=_trace)
    out = np.concatenate([np.asarray(r["y"], dtype=np.float32) for r in res.results], axis=0)
    if _trace:
        kernel._last = res
    return out
```

```python
import numpy as np
from contextlib import ExitStack
import concourse.bass as bass
import concourse.mybir as mybir
from concourse.bass_utils import run_bass_kernel_spmd

F32 = mybir.dt.float32
BF16 = mybir.dt.bfloat16
AF = mybir.ActivationFunctionType
ALU = mybir.AluOpType

D = 1024
S = 4096
T = 512
DFF = 2816
EPS = 1e-6
NCORES = 8
SEQ_PER_CORE = 2
NEG = -30000.0


class Buf:
    __slots__ = ("w", "r")

    def __init__(self):
        self.w = None
        self.r = {}


class Prog:
    ENG = ("pe", "act", "dve", "pool", "sp")

    def __init__(self):
        self.q = {e: [] for e in self.ENG}
        self.cnt = {e: 0 for e in self.ENG}
        self.dma_cnt = {}

    @staticmethod
    def _deps(reads, writes):
        deps = []
        for b in reads:
            if b.w is not None:
                deps.append(b.w)
        for b in writes:
            if b.w is not None:
                deps.append(b.w)
            deps.extend(b.r.items())
        return deps

    @staticmethod
    def _update(tok, reads, writes):
        for b in writes:
            b.w = tok
            b.r = {}
        for b in reads:
            if not any(b is w for w in writes):
                if b.r.get(tok[0], 0) < tok[1]:
                    b.r[tok[0]] = tok[1]

    def op(self, eng, fn, reads=(), writes=()):
        deps = self._deps(reads, writes)
        self.cnt[eng] += 1
        tok = (eng, self.cnt[eng])
        self.q[eng].append((fn, deps, None))
        self._update(tok, reads, writes)
        return tok

    def dma(self, eng, fn, sem, reads=(), writes=()):
        deps = self._deps(reads, writes)
        self.dma_cnt[sem] = self.dma_cnt.get(sem, 0) + 16
        tok = (sem, self.dma_cnt[sem])
        self.q[eng].append((fn, deps, sem))
        self._update(tok, reads, writes)
        return tok

    def dma_multi(self, eng, fns, sem, reads=(), writes=()):
        deps = self._deps(reads, writes)
        tok = None
        for fn in fns:
            self.dma_cnt[sem] = self.dma_cnt.get(sem, 0) + 16
            tok = (sem, self.dma_cnt[sem])
            self.q[eng].append((fn, deps, sem))
        self._update(tok, reads, writes)
        return tok

    def emit(self, nc, es):
        sems = {}
        for e in self.ENG:
            sems[e] = es.enter_context(nc.semaphore("s_" + e))
        for name in self.dma_cnt:
            sems[name] = es.enter_context(nc.semaphore("d_" + name))
        block = es.enter_context(nc.Block())

        needed = {e: set() for e in self.ENG}
        for engname in self.ENG:
            waited = {}
            for fn, deps, dsem in self.q[engname]:
                for (k, v) in deps:
                    if waited.get(k, 0) < v:
                        waited[k] = v
                        if k in needed:
                            needed[k].add(v)
        rank = {}
        for e in self.ENG:
            rank[e] = {v: i + 1 for i, v in enumerate(sorted(needed[e]))}
        self.n_inc = {e: len(needed[e]) for e in self.ENG}

        def run(engname, eng, final=False):
            waited = {}
            idx = 0
            for fn, deps, dsem in self.q[engname]:
                for (k, v) in deps:
                    if waited.get(k, 0) < v:
                        eng.wait_ge(sems[k], rank[k][v] if k in rank else v)
                        waited[k] = v
                ins = fn(eng)
                if dsem is None:
                    idx += 1
                    if idx in needed[engname]:
                        ins.then_inc(sems[engname], 1)
                else:
                    ins.then_inc(sems[dsem], 16)
            if final:
                for name, v in self.dma_cnt.items():
                    if waited.get(name, 0) < v:
                        eng.wait_ge(sems[name], v)

        @block.tensor
        def _(e):
            run("pe", e)

        @block.scalar
        def _(e):
            run("act", e)

        @block.vector
        def _(e):
            run("dve", e)

        @block.gpsimd
        def _(e):
            run("pool", e)

        @block.sync
        def _(e):
            run("sp", e, final=True)


def block_groups():
    specs = []

    def wspec(kind, **kw):
        d = dict(kind=kind)
        d.update(kw)
        specs.append(d)
        return len(specs) - 1

    d = {}
    d["win"] = {nm: wspec("cols", w="w_in", c0=512 * j) for nm, j in
                (("q", 3), ("k", 4), ("v", 5), ("b", 0), ("c", 1), ("u", 2))}
    d["woc"] = wspec("rows", w="w_o", r0=0, n=4, c0=0, ncol=1024)
    d["woa"] = [wspec("rows", w="w_o", r0=512, n=4, c0=512 * dh, ncol=512) for dh in range(2)]
    d["ffn"] = []
    for (f_lo, f_n) in ((0, 12), (12, 10)):
        gu = [wspec("gu", f0=(f_lo + 2 * j) * 128) for j in range(f_n // 2)]
        dn = []
        n1 = f_n // 2
        for dh in range(2):
            a = wspec("rows", w="w_dn", r0=f_lo * 128, n=n1, c0=512 * dh, ncol=512)
            b = wspec("rows", w="w_dn", r0=(f_lo + n1) * 128, n=f_n - n1, c0=512 * dh, ncol=512)
            dn.append((a, b, n1, f_n - n1))
        d["ffn"].append((f_lo, f_n, gu, dn))
    d["pg"] = [wspec("cols", w="w_pg", c0=512 * dh) for dh in range(2)]
    d["pp"] = wspec("rows", w="w_pp", r0=0, n=2, c0=0, ncol=1024)
    return specs, d


def group_len(g):
    return g["n"] * g["ncol"] if g["kind"] == "rows" else 4096


def pack_weights(ws):
    specs, _ = block_groups()
    out = np.zeros((len(specs), 128, 4096), np.float32)
    for i, g in enumerate(specs):
        k = g["kind"]
        if k == "cols":
            w = ws[g["w"]][:, g["c0"]:g["c0"] + 512]
            out[i] = w.reshape(8, 128, 512).transpose(1, 0, 2).reshape(128, 4096)
        elif k == "gu":
            w = ws["w_gu"]
            f0 = g["f0"]
            both = np.concatenate([w[:, f0:f0 + 256], w[:, DFF + f0:DFF + f0 + 256]], axis=1)
            out[i] = both.reshape(8, 128, 512).transpose(1, 0, 2).reshape(128, 4096)
        elif k == "rows":
            w = ws[g["w"]][g["r0"]:g["r0"] + g["n"] * 128, g["c0"]:g["c0"] + g["ncol"]]
            out[i, :, 0:g["n"] * g["ncol"]] = w.reshape(g["n"], 128, g["ncol"]).transpose(1, 0, 2).reshape(128, -1)
        elif k == "woa":
            w = ws["w_o"][512:1024, g["c0"]:g["c0"] + 512]
            out[i, 0:64, :] = w.reshape(8, 64, 512).transpose(1, 0, 2).reshape(64, 4096)
    return out


def build(nc, nblk=8, nseq=SEQ_PER_CORE):
    P = Prog()
    es = ExitStack()

    def dram(name, shape, kind="ExternalInput"):
        return nc.dram_tensor(name, list(shape), F32, kind=kind).ap()

    x_d = dram("x", [SEQ_PER_CORE, S, D])
    p_d = dram("p", [SEQ_PER_CORE, S, 256])
    bspecs, bplan = block_groups()
    NG = len(bspecs)
    wpack_d = dram("wpack", [NG, 128, 4096])
    wf_d = dram("wf", [128, 64])
    gpc_d = dram("gpc", [128, 24])
    gconv_d = dram("gconv", [128, 4])
    gattn_d = dram("gattn", [64, 8])
    cw_d = dram("cw", [128, 12])
    bfbc_d = dram("bfbc", [128, 8])
    gfin_d = dram("gfin", [128, D])
    bple_d = dram("bple", [128, D])
    ident_d = dram("ident", [128, 128])
    tri_d = dram("tri", [128, 128])
    ones_d = dram("ones", [128, 128])
    maskb_d = dram("maskb", [128, 128])
    bdiag_d = dram("bdiag", [128, 128])
    wn_d = dram("wn", [128, 128])
    onesrow_d = dram("onesrow", [128, 128])
    shiftm_d = dram("shiftm", [128, 128])
    sel_d = dram("sel", [128, 1024])
    y_d = dram("y", [SEQ_PER_CORE, S, D], kind="ExternalOutput")

    def sb(name, shape, dt):
        return es.enter_context(nc.sbuf_tensor(name, list(shape), dt))

    def psum(name):
        return es.enter_context(nc.psum_tensor(name, [128, 512], F32))

    identb = sb("identb", [128, 128], BF16)
    identf = sb("identf", [128, 128], F32)
    trif = sb("trif", [128, 128], F32)
    onesf = sb("onesf", [128, 128], F32)
    maskb = sb("maskb_s", [128, 128], BF16)
    bdiag = sb("bdiag_s", [128, 128], BF16)
    wn = sb("wn_s", [128, 128], BF16)
    sel = sb("sel_s", [128, 1024], BF16)
    bple = sb("bple_s", [128, D], BF16)
    onesrow = sb("onesrow_s", [128, 128], BF16)
    shiftm = sb("shiftm_s", [128, 128], BF16)
    wf = sb("wf_s", [128, 8, 8], BF16)
    gpc = sb("gpc_s", [128, 24], F32)
    gconv = sb("gconv_s", [128, 4], F32)
    gattn = sb("gattn_s", [64, 8], F32)
    cw = sb("cw_s", [128, 12], F32)
    bfbc = sb("bfbc_s", [128, 8], F32)
    gfin = sb("gfin_s", [128, D], F32)
    Bconst = Buf()

    def cdma(eng, out, in_):
        P.dma(eng, lambda e, out=out, in_=in_: e.dma_start(out=out, in_=in_), "c_" + eng)

    cdma("sp", identf[:], ident_d[:, :])
    cdma("sp", trif[:], tri_d[:, :])
    cdma("sp", onesf[:], ones_d[:, :])
    cdma("sp", gpc[:], gpc_d[:, :])
    cdma("sp", gconv[:], gconv_d[:, :])
    cdma("sp", gattn[:], gattn_d[:, :])
    cdma("sp", cw[:], cw_d[:, :])
    cdma("sp", bfbc[:], bfbc_d[:, :])
    cdma("sp", gfin[:], gfin_d[:, :])
    cdma("pool", identb[:], ident_d[:, :])
    cdma("pool", maskb[:], maskb_d[:, :])
    cdma("pool", bdiag[:], bdiag_d[:, :])
    cdma("pool", wn[:], wn_d[:, :])
    cdma("pool", sel[:], sel_d[:, :])
    cdma("pool", bple[:], bple_d[:, :])
    cdma("pool", onesrow[:], onesrow_d[:, :])
    cdma("pool", shiftm[:], shiftm_d[:, :])
    cdma("pool", wf[:], wf_d[:, :].rearrange("p (a b) -> p a b", a=8))
    Bc_sp, Bc_pool = Buf(), Buf()
    Bc_sp.w = ("c_sp", P.dma_cnt["c_sp"])
    Bc_pool.w = ("c_pool", P.dma_cnt["c_pool"])
    cdummy = sb("cdummy", [128, 4], F32)
    P.op("dve", lambda e: e.memset(cdummy[:], 0.0), reads=[Bc_sp, Bc_pool], writes=[Bconst])

    KT = sb("KT", [128, 4, S], BF16)
    VA = sb("VA", [128, 32, 8, 65], BF16)
    cK = sb("cK", [128, 32, 8], F32)
    carry = sb("carry", [128, 8], F32)
    ccar = sb("ccar", [128, 4, 2], F32)
    BKT = [Buf() for _ in range(4)]
    BVA = Buf()
    BcK = Buf()
    Bcarry = Buf()
    Bccar = Buf()

    hT = [[sb(f"h{b}_{t}", [128, D], F32) for t in range(4)] for b in range(2)]
    Bh = [[Buf() for _ in range(4)] for _ in range(2)]
    xnT = sb("xnT", [128, 8, T], BF16)
    BxnT = Buf()
    xnbf = [sb(f"xnbf{i}", [128, D], BF16) for i in range(2)]
    Bxnbf = [Buf(), Buf()]
    junk = sb("junk", [128, D], BF16)
    Bjunk = Buf()
    stat = sb("stat", [128, 16], F32)
    Bstat = [Buf() for _ in range(4)]
    fst = sb("fst", [128, 32], F32)
    Bfst = Buf()
    QT = sb("QT", [128, 8, T], BF16)
    BQT = [Buf() for _ in range(8)]
    crow = sb("crow", [128, T], BF16)
    Bcrow = Buf()
    NP = 4
    Pb = [sb(f"P{i}", [128, T], BF16) for i in range(NP)]
    BP = [Buf() for _ in range(NP)]
    yT = sb("yT", [128, 4, T], BF16)
    ByT = [Buf() for _ in range(4)]
    yTa = sb("yTa", [128, 4, T], BF16)
    ByTa = [Buf() for _ in range(8)]
    ytmp = sb("ytmp", [128, T], BF16)
    Bytmp = Buf()
    cu = sb("cu", [128, T + 2], F32)
    Bcu = Buf()
    acc = sb("acc", [128, T], F32)
    Bacc = Buf()
    rs = sb("rs", [128, T], F32)
    Brs = Buf()
    rsa, Brsa = rs, Brs
    csb, Bcsb = rs, Brs
    sqb = sb("sqb", [128, T], BF16)
    Bsqb = Buf()
    osq, Bosq = sqb, Bsqb
    NACT = 12
    actT = sb("actT", [128, NACT, T], BF16)
    BactT = [Buf() for _ in range(NACT)]
    sg = [sqb, sb("sg1", [128, T], BF16)]
    Bsg = [Bsqb, Buf()]
    pin = [sb(f"pin{i}", [128, 256], F32) for i in range(2)]
    Bpin = [Buf(), Buf()]
    pbf = sb("pbf", [128, 4, 256], BF16)
    Bpbf = [Buf() for _ in range(4)]
    pT = sb("pT", [128, 2, T], BF16)
    BpT = Buf()
    NSLOT = 4
    wslot = [sb(f"wslot{i}", [128, 4096], BF16) for i in range(NSLOT)]
    Bslot = [Buf() for _ in range(NSLOT)]

    PS = [psum(f"ps{i}") for i in range(8)]
    Bps = [Buf() for _ in range(8)]
    ring = {"mm": 0, "tt": 0, "s": 0, "o": 0, "g": 0, "sorder": [0, 1, 2, 3]}

    def bank(kind="mm"):
        if kind == "mm":
            b = ring["mm"] % 6
        elif kind == "tt":
            b = 6 + ring["tt"] % 2
        elif kind == "s":
            b = ring["sorder"][ring["s"] % 4]
        elif kind == "o":
            b = 4 + ring["o"] % 2
        else:
            b = 6 + ring["g"] % 2
        ring[kind] += 1
        return b

    P.op("dve", lambda e: e.memset(VA[:, :, :, 64:65], 1.0), writes=[BVA])
    P.op("dve", lambda e: e.memset(QT[:], 0.0), writes=BQT)
    P.op("dve", lambda e: e.memset(crow[:], 0.0), writes=[Bcrow])
    P.op("dve", lambda e: e.memset(osq[:], 0.0), writes=[Bosq])
    P.op("dve", lambda e: e.memset(rs[:], 1.0), writes=[Brs])
    P.op("dve", lambda e: e.memset(yTa[:], 0.0), writes=ByTa)
    P.op("dve", lambda e: e.memset(ytmp[:], 0.0), writes=[Bytmp])

    wstate = {"issued": 0}
    NGROUPS = NG * nseq * nblk
    Bchain = Buf()
    Bchain.w = Bc_pool.w

    def issue_group(i):
        g = bspecs[i % NG]
        s_ = i % NSLOT
        n = group_len(g)
        o = wslot[s_][:, 0:n]
        i_ = wpack_d[i % NG, :, 0:n]
        wr = [Bslot[s_], Bchain] if i < NSLOT else [Bslot[s_]]
        P.dma("pool", lambda e, o=o, i_=i_: e.dma_start(out=o, in_=i_), f"w{s_}", writes=wr)

    released = set()

    def wpump():
        while wstate["issued"] < NGROUPS:
            j = wstate["issued"]
            if j >= NSLOT and (j - NSLOT) not in released:
                break
            issue_group(j)
            wstate["issued"] += 1

    def wget(i):
        wpump()
        assert wstate["issued"] > i, (i, wstate["issued"])
        return wslot[i % NSLOT], Bslot[i % NSLOT]

    def wrel(*idx):
        for i in idx:
            released.add(i)
        wpump()

    def shift(v, off):
        if isinstance(v, dict):
            return {k: shift(x, off) for k, x in v.items()}
        if isinstance(v, list):
            return [shift(x, off) for x in v]
        return v

    plan = []
    for gb in range(nseq * nblk):
        off = gb * NG
        d = {}
        d["win"] = {k: v + off for k, v in bplan["win"].items()}
        d["woc"] = bplan["woc"] + off
        d["woa"] = [v + off for v in bplan["woa"]]
        d["ffn"] = [(f_lo, f_n, [v + off for v in gu], [(a + off, b + off, na, nb) for (a, b, na, nb) in dn])
                    for (f_lo, f_n, gu, dn) in bplan["ffn"]]
        d["pg"] = [v + off for v in bplan["pg"]]
        d["pp"] = bplan["pp"] + off
        plan.append(d)

    def load_x(sq, blk, hb):
        for tt in range(4):
            r0 = blk * T + tt * 128
            P.dma("sp", lambda e, o=hT[hb][tt][:], i_=x_d[sq, r0:r0 + 128, :]: e.dma_start(out=o, in_=i_),
                  f"h{hb}_{tt}", writes=[Bh[hb][tt]])

    def rstd_all(hb):
        for tt in range(4):
            h = hT[hb][tt]
            P.op("act", lambda e, h=h, tt=tt: e.activation(out=junk[:], in_=h[:], func=AF.Square,
                                                            accum_out=stat[:, tt:tt + 1]),
                 reads=[Bh[hb][tt]], writes=[Bjunk, Bstat[tt]])
        P.op("act", lambda e: e.activation(out=stat[:, 4:8], in_=stat[:, 0:4], func=AF.Ln,
                                           scale=1.0 / D, bias=EPS), reads=[], writes=Bstat)
        P.op("act", lambda e: e.activation(out=stat[:, 8:12], in_=stat[:, 4:8], func=AF.Exp, scale=-0.5),
             reads=[], writes=Bstat)

    def rstd_tiles(hb):
        for tt in range(4):
            h = hT[hb][tt]
            P.op("act", lambda e, h=h, tt=tt: e.activation(out=junk[:], in_=h[:], func=AF.Square,
                                                            accum_out=stat[:, tt:tt + 1]),
                 reads=[Bh[hb][tt]], writes=[Bjunk, Bstat[tt]])
            P.op("act", lambda e, tt=tt: e.activation(out=stat[:, 4 + tt:5 + tt], in_=stat[:, tt:tt + 1], func=AF.Ln,
                                                       scale=1.0 / D, bias=EPS), reads=[], writes=[Bstat[tt]])
            P.op("act", lambda e, tt=tt: e.activation(out=stat[:, 8 + tt:9 + tt], in_=stat[:, 4 + tt:5 + tt],
                                                       func=AF.Exp, scale=-0.5), reads=[], writes=[Bstat[tt]])

    def norm_to_xnT(hb, goff):
        rstd_tiles(hb)
        norm_tail(hb, goff)

    def norm_tail(hb, goff, pre_only=False, skip_pre=False):
        def xn(tt):
            h = hT[hb][tt]
            xb = xnbf[tt % 2]
            P.op("dve", lambda e, h=h, tt=tt, xb=xb: e.tensor_scalar(out=xb[:], in0=h[:],
                                                                      scalar1=stat[:, 8 + tt:9 + tt],
                                                                      scalar2=None, op0=ALU.mult),
                 reads=[Bh[hb][tt], Bstat[tt]], writes=[Bxnbf[tt % 2]])
        if not skip_pre:
            xn(0)
            xn(1)
        if pre_only:
            return
        for tt in range(4):
            xb = xnbf[tt % 2]
            tb = bank("tt")
            pbv = PS[tb][:].bitcast(BF16)

            def tr(e, xb=xb, pbv=pbv):
                ins = None
                for kc in range(8):
                    ins = e.transpose(out=pbv[:, kc * 128:(kc + 1) * 128], in_=xb[:, kc * 128:(kc + 1) * 128],
                                      identity=identb[:])
                return ins
            P.op("pe", tr, reads=[Bxnbf[tt % 2], Bconst], writes=[Bps[tb]])
            if tt + 2 < 4:
                xn(tt + 2)
            gb = gpc[:, goff:goff + 8].unsqueeze(2).to_broadcast([128, 8, 128])
            P.op("dve", lambda e, pbv=pbv, gb=gb, tt=tt: e.tensor_tensor(
                out=xnT[:, :, tt * 128:(tt + 1) * 128], in0=pbv.rearrange("p (k t) -> p k t", k=8), in1=gb,
                op=ALU.mult), reads=[Bps[tb], Bconst], writes=[BxnT])

    def mm_fm(b, wv, Bw, c0):
        def f(e):
            ins = None
            for kc in range(8):
                ins = e.matmul(PS[b][:], lhsT=wv[:, kc, c0:c0 + 128], rhs=xnT[:, kc, :],
                               start=(kc == 0), stop=(kc == 7))
            return ins
        P.op("pe", f, reads=[Bw, BxnT], writes=[Bps[b]])

    def w8(slot):
        return slot[:, 0:4096].rearrange("p (a b) -> p a b", a=8)

    gblk = 0
    load_x(0, 0, 0)
    for sq in range(nseq):
        for blk in range(nblk):
            hb = gblk % 2
            pl = plan[gblk]
            q0 = blk * T
            nxt = gblk + 1
            if nxt < nseq * nblk:
                load_x(nxt // nblk, nxt % nblk, nxt % 2)
            if blk == 0:
                P.op("dve", lambda e: e.memset(carry[:], 0.0), writes=[Bcarry])
                P.op("dve", lambda e: e.memset(ccar[:], 0.0), writes=[Bccar])

            if gblk == 0:
                norm_to_xnT(hb, 0)

            sQ, BQ = wget(pl["win"]["q"])
            for hp in range(4):
                b = bank()
                mm_fm(b, w8(sQ), BQ, hp * 128)
                P.op("act", lambda e, b=b, hp=hp: e.activation(out=QT[0:64, 2 * hp, :], in_=PS[b][0:64, :],
                                                                func=AF.Copy, scale=0.125),
                     reads=[Bps[b]], writes=[BQT[2 * hp]])
                P.op("act", lambda e, b=b, hp=hp: e.activation(out=QT[64:128, 2 * hp + 1, :], in_=PS[b][64:128, :],
                                                                func=AF.Copy, scale=0.125),
                     reads=[Bps[b]], writes=[BQT[2 * hp + 1]])
            wrel(pl["win"]["q"])
            sK, BK = wget(pl["win"]["k"])
            for hp in range(4):
                b = bank()
                mm_fm(b, w8(sK), BK, hp * 128)
                P.op("dve", lambda e, b=b, hp=hp, q0=q0: e.tensor_copy(out=KT[:, hp, q0:q0 + T], in_=PS[b][:]),
                     reads=[Bps[b]], writes=[BKT[hp]])
            wrel(pl["win"]["k"])
            sV, BV = wget(pl["win"]["v"])
            wv = w8(sV)
            def v_mm(tt):
                bv, bz = bank(), bank()

                def fv(e, tt=tt, bv=bv, bz=bz, wv=wv):
                    ins = None
                    for kc in range(8):
                        e.matmul(PS[bv][:], lhsT=xnT[:, kc, tt * 128:(tt + 1) * 128], rhs=wv[:, kc, :],
                                 start=(kc == 0), stop=(kc == 7))
                    for kc in range(8):
                        ins = e.matmul(PS[bz][:, 0:8], lhsT=xnT[:, kc, tt * 128:(tt + 1) * 128], rhs=wf[:, kc, :],
                                       start=(kc == 0), stop=(kc == 7))
                    return ins
                P.op("pe", fv, reads=[BxnT, BV, Bconst], writes=[Bps[bv], Bps[bz]])
                return bv, bz

            def v_post1(tt, bv, bz):
                kt = blk * 4 + tt
                P.op("dve", lambda e, kt=kt, bv=bv: e.tensor_copy(
                    out=VA[:, kt, :, 0:64], in_=PS[bv][:].rearrange("p (h d) -> p h d", h=8)),
                    reads=[Bps[bv]], writes=[BVA])
                P.op("dve", lambda e, bz=bz: e.tensor_tensor(out=fst[:, 0:8], in0=PS[bz][:, 0:8], in1=bfbc[:],
                                                              op=ALU.add),
                     reads=[Bps[bz], Bconst], writes=[Bfst])
                P.op("act", lambda e: e.activation(out=fst[:, 8:16], in_=fst[:, 0:8], func=AF.Exp, scale=-1.0),
                     reads=[], writes=[Bfst])
                P.op("act", lambda e: e.activation(out=fst[:, 16:24], in_=fst[:, 8:16], func=AF.Ln, bias=1.0),
                     reads=[], writes=[Bfst])

            def v_post2(tt):
                kt = blk * 4 + tt
                bc = bank()

                def fc_(e, bc=bc):
                    e.matmul(PS[bc][:, 0:8], lhsT=trif[:], rhs=fst[:, 16:24], start=True, stop=True)
                    return e.matmul(PS[bc][:, 8:16], lhsT=onesf[:], rhs=fst[:, 16:24], start=True, stop=True)
                P.op("pe", fc_, reads=[Bfst, Bconst], writes=[Bps[bc]])
                P.op("dve", lambda e, bc=bc, kt=kt: e.tensor_tensor(out=cK[:, kt, :], in0=PS[bc][:, 0:8],
                                                                    in1=carry[:], op=ALU.add),
                     reads=[Bps[bc], Bcarry], writes=[BcK])
                P.op("dve", lambda e, bc=bc: e.tensor_tensor(out=carry[:], in0=PS[bc][:, 8:16], in1=carry[:],
                                                              op=ALU.add),
                     reads=[Bps[bc]], writes=[Bcarry])

            vb = v_mm(0)
            for tt in range(4):
                v_post1(tt, *vb)
                if tt + 1 < 4:
                    vb = v_mm(tt + 1)
                v_post2(tt)
            wrel(pl["win"]["v"])
            bx = bank()

            def ftr(e, bx=bx, blk=blk):
                ins = None
                for tt in range(4):
                    ins = e.transpose(out=PS[bx][0:8, tt * 128:(tt + 1) * 128], in_=cK[:, blk * 4 + tt, :],
                                      identity=identf[:])
                return ins
            P.op("pe", ftr, reads=[BcK, Bconst], writes=[Bps[bx]])
            P.op("act", lambda e, bx=bx: e.activation(out=crow[0:8, :], in_=PS[bx][0:8, :], func=AF.Copy, scale=-1.0),
                 reads=[Bps[bx]], writes=[Bcrow])

            gb_, gc_, gu_ = pl["win"]["b"], pl["win"]["c"], pl["win"]["u"]
            sB, BB = wget(gb_)
            sC, BC = wget(gc_)
            sU, BU = wget(gu_)
            def conv_mm(c):
                b_b, b_c, b_u = bank(), bank(), bank()
                mm_fm(b_b, w8(sB), BB, c * 128)
                mm_fm(b_c, w8(sC), BC, c * 128)
                mm_fm(b_u, w8(sU), BU, c * 128)
                return b_b, b_c, b_u

            def conv_square():
                P.op("act", lambda e: e.activation(out=sqb[:], in_=acc[:], func=AF.Square),
                     reads=[Bacc], writes=[Bsqb])

            def conv_chain(c, banks, square=True):
                b_b, b_c, b_u = banks
                P.op("act", lambda e, b_c=b_c: e.activation(out=csb[:], in_=PS[b_c][:], func=AF.Copy),
                     reads=[Bps[b_c]], writes=[Bcsb])
                P.op("dve", lambda e, c=c: e.tensor_copy(out=cu[:, 0:2], in_=ccar[:, c, :]),
                     reads=[Bccar], writes=[Bcu])
                P.op("dve", lambda e, b_u=b_u: e.tensor_tensor(out=cu[:, 2:T + 2], in0=csb[:], in1=PS[b_u][:],
                                                                op=ALU.mult),
                     reads=[Bcsb, Bps[b_u]], writes=[Bcu])
                P.op("dve", lambda e, c=c: e.tensor_copy(out=ccar[:, c, :], in_=cu[:, T:T + 2]),
                     reads=[Bcu], writes=[Bccar])
                P.op("dve", lambda e, c=c: e.tensor_scalar(out=acc[:], in0=cu[:, 2:T + 2],
                                                           scalar1=cw[:, c * 3 + 2:c * 3 + 3], scalar2=None,
                                                           op0=ALU.mult),
                     reads=[Bcu, Bconst], writes=[Bacc])
                P.op("dve", lambda e, c=c: e.scalar_tensor_tensor(out=acc[:], in0=cu[:, 1:T + 1],
                                                                  scalar=cw[:, c * 3 + 1:c * 3 + 2], in1=acc[:],
                                                                  op0=ALU.mult, op1=ALU.add),
                     reads=[Bcu, Bconst], writes=[Bacc])
                P.op("dve", lambda e, c=c: e.scalar_tensor_tensor(out=acc[:], in0=cu[:, 0:T],
                                                                  scalar=cw[:, c * 3:c * 3 + 1], in1=acc[:],
                                                                  op0=ALU.mult, op1=ALU.add),
                     reads=[Bcu, Bconst], writes=[Bacc])
                P.op("dve", lambda e, b_b=b_b: e.tensor_tensor(out=acc[:], in0=acc[:], in1=PS[b_b][:], op=ALU.mult),
                     reads=[Bps[b_b]], writes=[Bacc])
                if square:
                    conv_square()

            def conv_fin(c):
                gbk = bank("g")
                P.op("pe", lambda e, gbk=gbk: e.matmul(PS[gbk][:], lhsT=bdiag[:], rhs=sqb[:], start=True, stop=True),
                     reads=[Bsqb, Bconst], writes=[Bps[gbk]])
                P.op("act", lambda e, gbk=gbk: e.activation(out=rs[:], in_=PS[gbk][:], func=AF.Ln, bias=EPS),
                     reads=[Bps[gbk]], writes=[Brs])
                P.op("act", lambda e: e.activation(out=rs[:], in_=rs[:], func=AF.Exp, scale=-0.5),
                     reads=[], writes=[Brs])
                P.op("dve", lambda e, c=c: e.scalar_tensor_tensor(out=yT[:, c, :], in0=acc[:],
                                                                  scalar=gconv[:, c:c + 1], in1=rs[:],
                                                                  op0=ALU.mult, op1=ALU.mult),
                     reads=[Bacc, Brs, Bconst], writes=[ByT[c]])

            cbanks = conv_mm(0)
            for c in range(3):
                conv_chain(c, cbanks)
                cbanks = conv_mm(c + 1)
                conv_fin(c)
            wrel(gb_, gc_, gu_)
            conv_chain(3, cbanks, square=False)
            ring["s"] = 0
            ring["sorder"] = [b for b in range(4) if b not in cbanks] + [b for b in range(4) if b in cbanks]
            conv_late = {2: conv_square, 4: (lambda: conv_fin(3))}
            units = []
            for h in range(8):
                lst = [(kt, 0, False) for kt in range(blk * 4)] + [(blk * 4 + j, 128 * j, True) for j in range(4)]
                for ui, (kt, c0, dg) in enumerate(lst):
                    units.append((h, kt, c0, dg, ui == 0, ui == len(lst) - 1))
            LOOK = 3
            obank = {}
            slots = {}
            deferred = []

            def post_pe(h, ob):
                gbk = bank("g")
                P.op("pe", lambda e, gbk=gbk: e.matmul(PS[gbk][:, :], lhsT=wn[:, :], rhs=osq[:, :],
                                                         start=True, stop=True),
                     reads=[Bosq, Bconst], writes=[Bps[gbk]])
                P.op("act", lambda e, gbk=gbk: e.activation(out=rsa[0:64, :], in_=PS[gbk][0:64, :], func=AF.Ln),
                     reads=[Bps[gbk]], writes=[Brsa])
                P.op("act", lambda e: e.activation(out=rsa[0:64, :], in_=rsa[0:64, :], func=AF.Exp, scale=-0.5),
                     reads=[], writes=[Brsa])
                if h % 2 == 0:
                    P.op("dve", lambda e, h=h, ob=ob: e.scalar_tensor_tensor(
                        out=yTa[0:64, h // 2, :], in0=PS[ob][0:64, :], scalar=gattn[:, h:h + 1], in1=rsa[0:64, :],
                        op0=ALU.mult, op1=ALU.mult), reads=[Bps[ob], Brsa, Bconst], writes=[ByTa[h]])
                else:
                    P.op("dve", lambda e, h=h, ob=ob: e.scalar_tensor_tensor(
                        out=ytmp[0:64, :], in0=PS[ob][0:64, :], scalar=gattn[:, h:h + 1], in1=rsa[0:64, :],
                        op0=ALU.mult, op1=ALU.mult), reads=[Bps[ob], Brsa, Bconst], writes=[Bytmp])

            def post_shift(h):
                if True:
                    g2 = bank("g")
                    P.op("pe", lambda e, g2=g2: e.matmul(PS[g2][:, :], lhsT=shiftm[:, :], rhs=ytmp[:, :],
                                                           start=True, stop=True),
                         reads=[Bytmp, Bconst], writes=[Bps[g2]])
                    P.op("act", lambda e, g2=g2, h=h: e.activation(out=yTa[64:128, h // 2, :],
                                                                    in_=PS[g2][64:128, :], func=AF.Copy),
                         reads=[Bps[g2]], writes=[ByTa[h]])

            for i in range(len(units) + LOOK):
                if i < len(units):
                    h, kt, c0, dg, first, last = units[i]
                    hp, s_ = h // 2, h % 2
                    rows = slice(64 * s_, 64 * s_ + 64)
                    sbk = bank("s")
                    pi = i % NP
                    slots[i] = pi

                    def fqk(e, sbk=sbk, hp=hp, rows=rows, kt=kt, c0=c0, dg=dg, h=h):
                        e.matmul(PS[sbk][:, c0:T], lhsT=KT[:, hp, kt * 128:(kt + 1) * 128],
                                 rhs=QT[:, h, c0:T], start=True, stop=False)
                        ins = e.matmul(PS[sbk][:, c0:T], lhsT=sel[:, h * 128:(h + 1) * 128], rhs=crow[:, c0:T],
                                       start=False, stop=(not dg))
                        if dg:
                            ins = e.matmul(PS[sbk][:, c0:c0 + 128], lhsT=identb[:], rhs=maskb[:],
                                           start=False, stop=True)
                        return ins
                    P.op("pe", fqk, reads=[BKT[hp], BQT[h], Bcrow, Bconst], writes=[Bps[sbk]])
                    P.op("act", lambda e, sbk=sbk, pi=pi, c0=c0, kt=kt, h=h: e.activation(
                        out=Pb[pi][:, c0:T], in_=PS[sbk][:, c0:T], func=AF.Exp, bias=cK[:, kt, h:h + 1], scale=1.0),
                        reads=[Bps[sbk], BcK], writes=[BP[pi]])
                j = i - LOOK
                if j >= 0:
                    h, kt, c0, dg, first, last = units[j]
                    if first:
                        obank[h] = bank("o")
                    ob = obank[h]
                    pi = slots[j]
                    P.op("pe", lambda e, ob=ob, kt=kt, h=h, pi=pi, c0=c0, first=first, last=last: e.matmul(
                        PS[ob][0:65, c0:T], lhsT=VA[:, kt, h, :], rhs=Pb[pi][:, c0:T], start=first, stop=last),
                        reads=[BP[pi], BVA], writes=[Bps[ob]])
                    if last:
                        P.op("act", lambda e, ob=ob: e.activation(out=osq[0:65, :], in_=PS[ob][0:65, :],
                                                                   func=AF.Square),
                             reads=[Bps[ob]], writes=[Bosq])
                        deferred.append((i + 2, (lambda h=h, ob=ob: post_pe(h, ob))))
                        if h % 2 == 1:
                            deferred.append((i + 5, (lambda h=h: post_shift(h))))
                if i in conv_late:
                    conv_late.pop(i)()
                while deferred and deferred[0][0] <= i:
                    deferred.pop(0)[1]()
            while deferred:
                deferred.pop(0)[1]()

            for tt in range(4):
                r0 = q0 + tt * 128
                P.dma("sp", lambda e, o=pin[tt % 2][:], i_=p_d[sq, r0:r0 + 128, :]: e.dma_start(out=o, in_=i_),
                      f"pin{tt % 2}", writes=[Bpin[tt % 2]])
                P.op("dve", lambda e, tt=tt: e.tensor_copy(out=pbf[:, tt, :], in_=pin[tt % 2][:]),
                     reads=[Bpin[tt % 2]], writes=[Bpbf[tt]])

            sWc, BWc = wget(pl["woc"])
            wcv = sWc[:, 0:4096].rearrange("p (a b) -> p a b", a=4)
            woa_ = [wget(pl["woa"][dh]) for dh in range(2)]
            for tt in range(4):
                for dh in range(2):
                    sWa, BWa = woa_[dh]
                    wav = sWa[:, 0:2048].rearrange("p (a b) -> p a b", a=4)
                    b = bank()

                    def fo(e, b=b, tt=tt, dh=dh, wav=wav, wcv=wcv):
                        for c in range(4):
                            e.matmul(PS[b][:], lhsT=yT[:, c, tt * 128:(tt + 1) * 128],
                                     rhs=wcv[:, c, dh * 512:(dh + 1) * 512], start=(c == 0), stop=False)
                        ins = None
                        for hp in range(3):
                            ins = e.matmul(PS[b][:], lhsT=yTa[:, hp, tt * 128:(tt + 1) * 128], rhs=wav[:, hp, :],
                                           start=False, stop=False)
                        return ins

                    def fo2(e, b=b, tt=tt, wav=wav):
                        ins = None
                        ins = e.matmul(PS[b][:], lhsT=yTa[:, 3, tt * 128:(tt + 1) * 128], rhs=wav[:, 3, :],
                                       start=False, stop=True)
                        return ins
                    P.op("pe", fo, reads=ByT + ByTa[0:6] + [BWc, BWa], writes=[Bps[b]])
                    P.op("pe", fo2, reads=ByTa[6:8] + [BWa], writes=[Bps[b]])
                    hh = hT[hb][tt]
                    P.op("dve", lambda e, b=b, hh=hh, dh=dh: e.tensor_tensor(
                        out=hh[:, dh * 512:(dh + 1) * 512], in0=hh[:, dh * 512:(dh + 1) * 512], in1=PS[b][:],
                        op=ALU.add), reads=[Bps[b]], writes=[Bh[hb][tt]])
            wrel(pl["woa"][0], pl["woa"][1], pl["woc"])

            norm_to_xnT(hb, 8)
            for (f_lo, f_n, gu, dn) in pl["ffn"]:
                for j in range(f_n):
                    sG, BG = wget(gu[j // 2])
                    gv = w8(sG)
                    bg, bu = bank(), bank()
                    mm_fm(bg, gv, BG, (j % 2) * 128)
                    mm_fm(bu, gv, BG, 256 + (j % 2) * 128)
                    si = j % 2
                    P.op("act", lambda e, bg=bg, si=si: e.activation(out=sg[si][:], in_=PS[bg][:], func=AF.Silu),
                         reads=[Bps[bg]], writes=[Bsg[si]])
                    P.op("dve", lambda e, bu=bu, si=si, j=j: e.tensor_tensor(out=actT[:, j, :], in0=sg[si][:],
                                                                              in1=PS[bu][:], op=ALU.mult),
                         reads=[Bsg[si], Bps[bu]], writes=[BactT[j]])
                    if j % 2 == 1:
                        wrel(gu[j // 2])
                for dh in range(2):
                    ga, gb2, na, nb = dn[dh]
                    sA, BA = wget(ga)
                    sB2, BB2 = wget(gb2)
                    av = sA[:, 0:na * 512].rearrange("p (a b) -> p a b", a=na)
                    bv_ = sB2[:, 0:nb * 512].rearrange("p (a b) -> p a b", a=nb)
                    for tt in range(4):
                        b = bank()

                        def fd(e, b=b, tt=tt, av=av, bv_=bv_, na=na, nb=nb):
                            ins = None
                            for j in range(na + nb):
                                rhs = av[:, j, :] if j < na else bv_[:, j - na, :]
                                ins = e.matmul(PS[b][:], lhsT=actT[:, j, tt * 128:(tt + 1) * 128], rhs=rhs,
                                               start=(j == 0), stop=(j == na + nb - 1))
                            return ins
                        P.op("pe", fd, reads=BactT[0:f_n] + [BA, BB2], writes=[Bps[b]])
                        hh = hT[hb][tt]
                        P.op("dve", lambda e, b=b, hh=hh, dh=dh: e.tensor_tensor(
                            out=hh[:, dh * 512:(dh + 1) * 512], in0=hh[:, dh * 512:(dh + 1) * 512], in1=PS[b][:],
                            op=ALU.add), reads=[Bps[b]], writes=[Bh[hb][tt]])
                    wrel(ga, gb2)

            norm_to_xnT(hb, 16)
            tb = bank("tt")
            pbv = PS[tb][:].bitcast(BF16)

            def ftp(e, pbv=pbv):
                ins = None
                for pc in range(2):
                    for tt in range(4):
                        o0 = (pc * 4 + tt) * 128
                        ins = e.transpose(out=pbv[:, o0:o0 + 128], in_=pbf[:, tt, pc * 128:(pc + 1) * 128],
                                          identity=identb[:])
                return ins
            P.op("pe", ftp, reads=Bpbf + [Bconst], writes=[Bps[tb]])
            P.op("dve", lambda e, pbv=pbv: e.tensor_copy(out=pT[:, :, :], in_=pbv.rearrange("p (c t) -> p c t", c=2)),
                 reads=[Bps[tb]], writes=[BpT])
            if nxt < nseq * nblk:
                rstd_all(nxt % 2)
                norm_tail(nxt % 2, 0, pre_only=True)
            sPP, BPP = None, None
            for dh in range(2):
                sPG, BPG = wget(pl["pg"][dh])
                if dh == 0:
                    sPP, BPP = wget(pl["pp"])
                pgv = w8(sPG)
                ppv = sPP[:, 0:2048].rearrange("p (a b) -> p a b", a=2)
                for tt in range(4):
                    bgt, bpp = bank(), bank()

                    def fg(e, bgt=bgt, bpp=bpp, tt=tt, dh=dh, pgv=pgv, ppv=ppv):
                        for kc in range(8):
                            e.matmul(PS[bgt][:], lhsT=xnT[:, kc, tt * 128:(tt + 1) * 128], rhs=pgv[:, kc, :],
                                     start=(kc == 0), stop=False)
                        e.matmul(PS[bgt][:], lhsT=onesrow[:, :], rhs=bple[:, dh * 512:(dh + 1) * 512],
                                 start=False, stop=True)
                        ins = None
                        for pc in range(2):
                            ins = e.matmul(PS[bpp][:], lhsT=pT[:, pc, tt * 128:(tt + 1) * 128],
                                           rhs=ppv[:, pc, dh * 512:(dh + 1) * 512], start=(pc == 0), stop=(pc == 1))
                        return ins
                    P.op("pe", fg, reads=[BxnT, BpT, BPG, BPP, Bconst], writes=[Bps[bgt], Bps[bpp]])
                    P.op("act", lambda e, bgt=bgt: e.activation(out=acc[:], in_=PS[bgt][:], func=AF.Sigmoid),
                         reads=[Bps[bgt]], writes=[Bacc])
                    P.op("dve", lambda e, bpp=bpp: e.tensor_tensor(out=acc[:], in0=acc[:], in1=PS[bpp][:],
                                                                    op=ALU.mult),
                         reads=[Bps[bpp]], writes=[Bacc])
                    hh = hT[hb][tt]
                    P.op("dve", lambda e, hh=hh, dh=dh: e.tensor_tensor(
                        out=hh[:, dh * 512:(dh + 1) * 512], in0=hh[:, dh * 512:(dh + 1) * 512], in1=acc[:],
                        op=ALU.add), reads=[Bacc], writes=[Bh[hb][tt]])
                wrel(pl["pg"][dh])
            wrel(pl["pp"])

            if nxt < nseq * nblk:
                norm_tail(nxt % 2, 0, skip_pre=True)

            rstd_all(hb)
            for tt in range(4):
                h = hT[hb][tt]
                P.op("dve", lambda e, h=h, tt=tt: e.scalar_tensor_tensor(out=h[:], in0=h[:],
                                                                          scalar=stat[:, 8 + tt:9 + tt],
                                                                          in1=gfin[:], op0=ALU.mult, op1=ALU.mult),
                     reads=[Bstat[tt], Bconst], writes=[Bh[hb][tt]])
                r0 = q0 + tt * 128
                P.dma("sp", lambda e, h=h, o=y_d[sq, r0:r0 + 128, :]: e.dma_start(out=o, in_=h[:]),
                      f"h{hb}_{tt}", reads=[Bh[hb][tt]])
            gblk += 1

    P.emit(nc, es)
    P.sbuf_left = nc.sbuf_bytes_remaining
    es.close()
    return P


def _host_consts():
    c = {}
    c["ident"] = np.eye(128, dtype=np.float32)
    j = np.arange(128)
    c["tri"] = (j[:, None] <= j[None, :]).astype(np.float32)
    c["ones"] = np.ones((128, 128), np.float32)
    c["maskb"] = np.where(j[:, None] <= j[None, :], 0.0, NEG).astype(np.float32)
    c["bdiag"] = ((j[:, None] // 64) == (j[None, :] // 64)).astype(np.float32) / 64.0
    wn = np.zeros((128, 128), np.float32)
    wn[0:64, 0:64] = 1.0 / 64.0
    wn[64, 0:64] = EPS
    c["wn"] = wn
    sel = np.zeros((128, 8, 128), np.float32)
    for h in range(8):
        sel[h, h, :] = 1.0
    c["sel"] = sel.reshape(128, 1024)
    orow = np.zeros((128, 128), np.float32)
    orow[0, :] = 1.0
    c["onesrow"] = orow
    sh = np.zeros((128, 128), np.float32)
    for d_ in range(64):
        sh[d_, 64 + d_] = 1.0
    c["shiftm"] = sh
    return c


_NC_CACHE = {}


def kernel(x, p, mix_norm, w_in, b_f, conv_w, mix_out_norm, w_o, ffn_norm, w_gate_up, w_down,
           ple_norm, w_ple_gate, b_ple_gate, w_ple_proj, final_norm, _nblk=8, _trace=False):
    f = lambda a: np.ascontiguousarray(np.asarray(a, dtype=np.float32))
    x = f(x); p = f(p)
    ws = {"w_in": f(w_in[0]), "w_o": f(w_o[0]), "w_gu": f(w_gate_up[0]), "w_dn": f(w_down[0]),
          "w_pg": f(w_ple_gate[0]), "w_pp": f(w_ple_proj[0])}
    shared = {"wpack": pack_weights(ws),
              "wf": f(ws["w_in"][:, 3072:3080].reshape(8, 128, 8).transpose(1, 0, 2).reshape(128, 64))}
    pc = lambda g: f(np.asarray(g).reshape(8, 128).T)
    shared["gpc"] = f(np.concatenate([pc(mix_norm[0]), pc(ffn_norm[0]), pc(ple_norm[0])], axis=1))
    go = np.asarray(mix_out_norm[0])
    shared["gconv"] = f(go[0:512].reshape(4, 128).T)
    shared["gattn"] = f(go[512:1024].reshape(8, 64).T)
    shared["cw"] = f(np.asarray(conv_w[0]).reshape(3, 4, 128).transpose(2, 1, 0).reshape(128, 12))
    shared["bfbc"] = f(np.broadcast_to(np.asarray(b_f[0])[None, :], (128, 8)))
    shared["gfin"] = f(np.broadcast_to(np.asarray(final_norm)[None, :], (128, D)))
    bp = np.zeros((128, D), np.float32)
    bp[0, :] = np.asarray(b_ple_gate[0])
    shared["bple"] = bp
    shared.update(_host_consts())

    key = _nblk
    if key not in _NC_CACHE:
        nc = bass.Bass("TRN2", target_bir_lowering=False)
        build(nc, nblk=_nblk)
        _NC_CACHE[key] = nc
    nc = _NC_CACHE[key]
    in_maps = []
    for c in range(NCORES):
        m = dict(shared)
        m["x"] = f(x[2 * c:2 * c + 2])
        m["p"] = f(p[0, 2 * c:2 * c + 2])
        in_maps.append(m)
    res = run_bass_kernel_spmd(nc, in_maps, core_ids=list(range(NCORES)), trace=_trace)
    out = np.concatenate([np.asarray(r["y"], dtype=np.float32) for r in res.results], axis=0)
    if _trace:
        kernel._last = res
    return out
```
